# Optimizing a Trainium2 kernel written in Bass

```python
import jax, jax.numpy as jnp
from jax import lax
import numpy as np

D_MODEL = 1024
BATCH = 4
SEQ = 4096
DEPTH = 4
DEC_BATCH = 8
DEC_SEQ = 8192
PAST_LEN = 128

HEAD_DIM = 64
A_HEADS = 4
A_WIDTH = A_HEADS * HEAD_DIM
GLA_CHUNK = 64
B_HEADS = 6
B_KV_HEADS = 2
B_GROUP = B_HEADS // B_KV_HEADS
B_HALF_WINDOW = 128
C_PAIRS = ((128, 1), (512, 4), (2048, 16))
C_HEADS_PER_PAIR = 2
C_HEADS = C_HEADS_PER_PAIR * len(C_PAIRS)
IN_SPLITS = (A_WIDTH, A_WIDTH, A_WIDTH, A_WIDTH, A_WIDTH,
             B_HEADS * HEAD_DIM, B_KV_HEADS * HEAD_DIM, B_KV_HEADS * HEAD_DIM,
             C_HEADS * HEAD_DIM, C_HEADS * HEAD_DIM, C_HEADS * HEAD_DIM)
IN_WIDTH = 5 * A_WIDTH + (B_HEADS + 2 * B_KV_HEADS) * HEAD_DIM + 3 * C_HEADS * HEAD_DIM
OUT_WIDTH = A_WIDTH + B_HEADS * HEAD_DIM + C_HEADS_PER_PAIR * HEAD_DIM
D_FF = -(-(8 * D_MODEL) // (3 * 256)) * 256
ROPE_THETA = 10000.0
NORM_EPS = 1e-6
MASK_VALUE = -1e30
GATE_FLOOR = 1e-30

kernel_name = "hymba_style_bidir_hybrid_encoder"


def rmsnorm(x, g):
    xf = x.astype(jnp.float32)
    y = xf * lax.rsqrt(jnp.mean(xf * xf, axis=-1, keepdims=True) + NORM_EPS) * g.astype(jnp.float32)
    return y.astype(x.dtype)


def rope(x, pos):
    half = x.shape[-1] // 2
    inv = ROPE_THETA ** (-jnp.arange(half, dtype=jnp.float32) / half)
    ang = pos.astype(jnp.float32)[:, None] * inv[None, :]
    cos = jnp.cos(ang)[None, :, None, :]
    sin = jnp.sin(ang)[None, :, None, :]
    xf = x.astype(jnp.float32)
    x1, x2 = xf[..., :half], xf[..., half:]
    return jnp.concatenate([x1 * cos - x2 * sin, x2 * cos + x1 * sin], axis=-1).astype(x.dtype)


def gla_chunk_scan(q, k, v, g):
    Bn, T, H, K = q.shape
    V = v.shape[-1]
    n = T // GLA_CHUNK

    def to_chunks(a):
        return a.astype(jnp.float32).reshape(Bn, n, GLA_CHUNK, H, a.shape[-1]).transpose(1, 0, 3, 2, 4)

    qc, kc, vc = to_chunks(q), to_chunks(k), to_chunks(v)
    bc = jnp.cumsum(to_chunks(g), axis=3)
    causal_in_chunk = jnp.tril(jnp.ones((GLA_CHUNK, GLA_CHUNK), dtype=bool))[None, None, :, :, None]

    def step(S, inp):
        q_, k_, v_, b_ = inp
        diff = b_[:, :, :, None, :] - b_[:, :, None, :, :]
        dec = jnp.where(causal_in_chunk, jnp.exp(jnp.where(causal_in_chunk, diff, 0.0)), 0.0)
        A = jnp.einsum('bhtk,bhtsk,bhsk->bhts', q_, dec, k_)
        o = jnp.einsum('bhts,bhsv->bhtv', A, v_) + jnp.einsum('bhtk,bhkv->bhtv', q_ * jnp.exp(b_), S)
        b_last = b_[:, :, -1:, :]
        S = jnp.exp(b_last[:, :, 0, :])[..., None] * S + jnp.einsum('bhsk,bhsv->bhkv', k_ * jnp.exp(b_last - b_), v_)
        return S, o

    S0 = jnp.zeros((Bn, H, K, V), jnp.float32)
    _, o = lax.scan(step, S0, (qc, kc, vc, bc))
    return o.transpose(1, 0, 3, 2, 4).reshape(Bn, T, H, V)


def hgrn2_bidir(q, i, f_fwd, f_bwd, gate, lb, norm_g):
    lb = lb.reshape(2, A_HEADS, HEAD_DIM).astype(jnp.float32)

    def forget(f_raw, lb_d):
        s = jax.nn.sigmoid(f_raw.astype(jnp.float32))
        f = lb_d + (1.0 - lb_d) * s
        return jnp.log(jnp.maximum(f, GATE_FLOOR)), (1.0 - lb_d) * (1.0 - s)

    g_f, k_f = forget(f_fwd, lb[0])
    g_b, k_b = forget(f_bwd, lb[1])
    flip = lambda a: jnp.flip(a, axis=1)
    o = gla_chunk_scan(q, k_f, i, g_f) + flip(gla_chunk_scan(flip(q), flip(k_b), flip(i), flip(g_b)))
    o = rmsnorm(o, norm_g) * jax.nn.silu(gate.astype(jnp.float32))
    return o.reshape(o.shape[0], o.shape[1], A_WIDTH)


def banded_attention(q, k, v, w, sink):
    N, Hk, G, L, dh = q.shape
    nb = -(-L // w)
    pad = nb * w - L
    qb = jnp.pad(q, ((0, 0), (0, 0), (0, 0), (0, pad), (0, 0))).reshape(N, Hk, G, nb, w, dh)
    kp = jnp.pad(k, ((0, 0), (0, 0), (w, pad + w), (0, 0))).reshape(N, Hk, nb + 2, w, dh)
    vp = jnp.pad(v, ((0, 0), (0, 0), (w, pad + w), (0, 0))).reshape(N, Hk, nb + 2, w, dh)
    kw = jnp.concatenate([kp[:, :, :-2], kp[:, :, 1:-1], kp[:, :, 2:]], axis=3)
    vw = jnp.concatenate([vp[:, :, :-2], vp[:, :, 1:-1], vp[:, :, 2:]], axis=3)
    s = jnp.einsum('nhgbqd,nhbkd->nhgbqk', qb, kw, preferred_element_type=jnp.float32) * (dh ** -0.5)
    a = jnp.arange(w)[:, None]
    c = jnp.arange(3 * w)[None, :]
    band = jnp.abs(c - w - a) <= w
    keypos = jnp.arange(nb)[:, None] * w - w + jnp.arange(3 * w)[None, :]
    valid = (keypos >= 0) & (keypos < L)
    mask = band[None, :, :] & valid[:, None, :]
    s = jnp.where(mask, s, MASK_VALUE)
    m = jnp.max(s, axis=-1)
    if sink is not None:
        sk = sink.astype(jnp.float32)[None, :, :, None, None]
        m = jnp.maximum(m, sk)
    p = jnp.where(mask, jnp.exp(s - m[..., None]), 0.0)
    denom = jnp.sum(p, axis=-1)
    if sink is not None:
        denom = denom + jnp.exp(sk - m)
    o = jnp.einsum('nhgbqk,nhbkd->nhgbqd', p, vw.astype(jnp.float32)) / denom[..., None]
    lse = m + jnp.log(denom)
    o = o.reshape(N, Hk, G, nb * w, dh)[:, :, :, :L]
    lse = lse.reshape(N, Hk, G, nb * w)[:, :, :, :L]
    return o, lse


def windowed_gqa(q, k, v, qn_g, kn_g, sink, pos):
    Bn, T = q.shape[:2]
    q = rope(rmsnorm(q, qn_g), pos)
    k = rope(rmsnorm(k, kn_g), pos)
    qg = q.reshape(Bn, T, B_KV_HEADS, B_GROUP, HEAD_DIM).transpose(0, 2, 3, 1, 4)
    o, _ = banded_attention(qg, k.transpose(0, 2, 1, 3), v.transpose(0, 2, 1, 3), B_HALF_WINDOW,
                            sink.reshape(B_KV_HEADS, B_GROUP))
    return o.transpose(0, 3, 1, 2, 4).reshape(Bn, T, B_HEADS * HEAD_DIM)


def to_strided(a, d):
    Bn, H, T = a.shape[:3]
    rest = a.shape[3:]
    a = jnp.moveaxis(a.reshape(Bn, H, T // d, d, *rest), 3, 1)
    return a.reshape(Bn * d, H, T // d, *rest)


def from_strided(a, Bn, d):
    H, L = a.shape[1:3]
    rest = a.shape[3:]
    a = jnp.moveaxis(a.reshape(Bn, d, H, L, *rest), 1, 3)
    return a.reshape(Bn, H, L * d, *rest)


def dilated_attention(q, k, v, qn_g, kn_g, pos):
    Bn, T = q.shape[:2]
    q = rope(rmsnorm(q, qn_g), pos).transpose(0, 2, 1, 3)
    k = rope(rmsnorm(k, kn_g), pos).transpose(0, 2, 1, 3)
    v = v.transpose(0, 2, 1, 3)
    outs, lses = [], []
    for g, (window, dil) in enumerate(C_PAIRS):
        sl = slice(g * C_HEADS_PER_PAIR, (g + 1) * C_HEADS_PER_PAIR)
        qd, kd, vd = to_strided(q[:, sl], dil), to_strided(k[:, sl], dil), to_strided(v[:, sl], dil)
        o, lse = banded_attention(qd[:, :, None], kd, vd, window // (2 * dil), None)
        outs.append(from_strided(o[:, :, 0], Bn, dil))
        lses.append(from_strided(lse[:, :, 0], Bn, dil))
    alpha = jax.nn.softmax(jnp.stack(lses), axis=0)
    o = jnp.einsum('gbhtd,gbht->bhtd', jnp.stack(outs), alpha)
    return o.transpose(0, 2, 1, 3).reshape(Bn, T, C_HEADS_PER_PAIR * HEAD_DIM)


def encoder_layer(x, n1, w_in_l, lb_l, a_g, bq_g, bk_g, sink_l, cq_g, ck_g, w_out_l, n2, wg, wu, wd):
    Bn, T, _ = x.shape
    xn = rmsnorm(x, n1)
    proj = xn @ w_in_l
    offsets = np.cumsum(np.array(IN_SPLITS))[:-1]
    aq, aff, afb, ai, ag, bq, bk, bv, cq, ck, cv = jnp.split(proj, offsets, axis=-1)
    hd = lambda a: a.reshape(Bn, T, -1, HEAD_DIM)
    pos = jnp.arange(T)
    ya = hgrn2_bidir(hd(aq), hd(ai), hd(aff), hd(afb), hd(ag), lb_l, a_g).astype(x.dtype)
    yb = windowed_gqa(hd(bq), hd(bk), hd(bv), bq_g, bk_g, sink_l, pos).astype(x.dtype)
    yc = dilated_attention(hd(cq), hd(ck), hd(cv), cq_g, ck_g, pos).astype(x.dtype)
    h = x + jnp.concatenate([ya, yb, yc], axis=-1) @ w_out_l
    hn = rmsnorm(h, n2)
    return h + (jax.nn.silu(hn @ wg) * (hn @ wu)) @ wd


def setup_inputs(seed: int = 0) -> dict:
    key = jax.random.key(seed)
    ks = jax.random.split(key, 16)
    nrm = lambda k, s: jax.random.normal(k, s, jnp.float32)
    return {
        "x_prompt": nrm(ks[0], (BATCH, SEQ, D_MODEL)),
        "x_sample": nrm(ks[1], (DEC_BATCH, DEC_SEQ, D_MODEL)),
        "norm1_g": 1.0 + 0.02 * nrm(ks[2], (DEPTH, D_MODEL)),
        "w_in": nrm(ks[3], (DEPTH, D_MODEL, IN_WIDTH)) * D_MODEL ** -0.5,
        "lb_raw": 0.1 * nrm(ks[4], (DEPTH, 2, A_WIDTH)),
        "a_norm_g": 1.0 + 0.02 * nrm(ks[5], (DEPTH, HEAD_DIM)),
        "b_qn_g": 1.0 + 0.02 * nrm(ks[6], (DEPTH, HEAD_DIM)),
        "b_kn_g": 1.0 + 0.02 * nrm(ks[7], (DEPTH, HEAD_DIM)),
        "b_sink": 0.5 * nrm(ks[8], (DEPTH, B_HEADS)),
        "c_qn_g": 1.0 + 0.02 * nrm(ks[9], (DEPTH, HEAD_DIM)),
        "c_kn_g": 1.0 + 0.02 * nrm(ks[10], (DEPTH, HEAD_DIM)),
        "w_out": nrm(ks[11], (DEPTH, OUT_WIDTH, D_MODEL)) * (0.5 * OUT_WIDTH ** -0.5),
        "norm2_g": 1.0 + 0.02 * nrm(ks[12], (DEPTH, D_MODEL)),
        "w_gate": nrm(ks[13], (DEPTH, D_MODEL, D_FF)) * D_MODEL ** -0.5,
        "w_up": nrm(ks[14], (DEPTH, D_MODEL, D_FF)) * D_MODEL ** -0.5,
        "w_down": nrm(ks[15], (DEPTH, D_FF, D_MODEL)) * (0.5 * D_FF ** -0.5),
    }


def reference(x_prompt, x_sample, norm1_g, w_in, lb_raw, a_norm_g, b_qn_g, b_kn_g, b_sink,
              c_qn_g, c_kn_g, w_out, norm2_g, w_gate, w_up, w_down):
    lb_sm = jax.nn.softmax(lb_raw.astype(jnp.float32), axis=0)
    lb = jnp.cumsum(lb_sm, axis=0) - lb_sm[:1]

    def trunk(x):
        for l in range(DEPTH):
            x = encoder_layer(x, norm1_g[l], w_in[l], lb[l], a_norm_g[l], b_qn_g[l], b_kn_g[l], b_sink[l],
                              c_qn_g[l], c_kn_g[l], w_out[l], norm2_g[l], w_gate[l], w_up[l], w_down[l])
        return x

    y_prompt = trunk(x_prompt)
    y_sample = trunk(x_sample)
    return (y_prompt, y_sample)
```

```python
import numpy as np
import concourse.bass as bass
import concourse.mybir as mybir
from concourse.bass_utils import run_bass_kernel_spmd
from contextlib import ExitStack

F32 = mybir.dt.float32
BF16 = mybir.dt.bfloat16
AF = mybir.ActivationFunctionType
ALU = mybir.AluOpType
AX = mybir.AxisListType

import os
KSTOP = int(os.environ.get("KSTOP", "0"))
D = 1024
DFF = 2816
NFF = DFF // 128
INW = 3072
EPS = 1e-6
FLOOR = 1e-30
OTW = 4352
O_DIR = 1024
O_V = 2048
O_GATE = 2304
O_BQ = 2560
O_BK = 2944
O_BV = 3072
O_C = 3200


class Buf:
    __slots__ = ("name", "t", "w", "r", "dsem", "dcnt", "psum")

    def __init__(self, name, t, psum=False):
        self.psum = psum
        self.name = name
        self.t = t
        self.w = None
        self.r = []
        self.dsem = None
        self.dcnt = 0

    def __getitem__(self, idx):
        return self.t[idx]


class Sched:
    ENG = ("pe", "act", "dve", "pool", "sp")

    def __init__(self, nc, es):
        self.nc = nc
        self.es = es
        self.sems = {}
        self.cnt = {}
        for e in self.ENG:
            self.sems[e] = es.enter_context(nc.semaphore("prog_" + e))
            self.cnt[e] = 0
        self.ndsem = 0
        self.free_dsem = {}
        self.dsem_eng = {}
        for eng, cnt in (("sp", 48), ("pool", 24)):
            fl = self.free_dsem.setdefault(eng, [])
            for _ in range(cnt):
                key = "d%s%d" % (eng, self.ndsem)
                self.ndsem += 1
                self.sems[key] = es.enter_context(nc.semaphore("dma_%s" % key))
                self.dsem_eng[key] = eng
                fl.append((key, 0))
        allsems = list(self.sems.values())
        with nc.Block() as block:
            @block.sync
            def _(e):
                for sm in allsems:
                    e.sem_clear(sm)
        self.epoch = 0
        self.known = {e: {} for e in self.ENG}
        self.q = {e: [] for e in self.ENG}
        self.dma_tokens = {}
        self.dma_all = {}
        self.ninst = 0

    def _dsem(self, buf, eng):
        if buf.dsem is None:
            fl = self.free_dsem.setdefault(eng, [])
            assert fl, "out of pre-allocated DMA semaphores for " + eng
            buf.dsem, buf.dcnt = fl.pop()
        assert self.dsem_eng[buf.dsem] == eng, "buffer DMA'd from two queues"
        return buf.dsem

    def release(self, bufs):
        for b in bufs:
            if b.dsem is not None:
                self.free_dsem[self.dsem_eng[b.dsem]].append((b.dsem, b.dcnt))
                b.dsem = None

    def _waits(self, eng, deps):
        kn = self.known[eng]
        need = {}
        for d in deps:
            if d is None:
                continue
            k, v, ep = d
            if k in self.cnt and ep != self.epoch:
                continue
            if kn.get(k, -1) < v and need.get(k, -1) < v:
                need[k] = v
        out = []
        for k, v in need.items():
            kn[k] = v
            out.append((k, v))
            if k in self.cnt:
                self.q[k][v]["signal"] = True
        return out

    def op(self, eng, fn, reads=(), writes=()):
        deps = []
        own_raw = set()
        for b in reads:
            deps.append(b.w)
            if b.w is not None and b.w[0] == eng:
                own_raw.add(b.w)
            if b.psum:
                deps.extend(b.r)
        for b in writes:
            deps.append(b.w)
            if b.w is not None and b.w[0] == eng and eng != "pe":
                own_raw.add(b.w)
            deps.extend(b.r)
        deps2 = [d for d in deps if d is not None and (d[0] != eng or d in own_raw)]
        w = self._waits(eng, deps2)
        idx = len(self.q[eng])
        tok = (eng, idx, self.epoch)
        self.q[eng].append({"kind": "op", "fn": fn, "waits": w, "signal": False})
        self.ninst += 1
        for b in reads:
            b.r.append(tok)
        for b in writes:
            b.w = tok
            b.r = []
        return tok

    def dma(self, eng, out, in_, sb, load, **kw):
        deps = [sb.w]
        if load:
            deps.extend(sb.r)
        w = self._waits(eng, [d for d in deps if d is not None])
        key = self._dsem(sb, eng)
        sb.dcnt += 16
        tok = (key, sb.dcnt, self.epoch)
        self.q[eng].append({"kind": "dma", "out": out, "in_": in_, "key": key, "kw": kw, "waits": w, "signal": False})
        self.dma_tokens[key] = sb.dcnt
        self.dma_all[key] = sb.dcnt
        self.ninst += 1
        if load:
            sb.w = tok
            sb.r = []
        else:
            sb.r.append(tok)
        return tok

    def flush(self):
        nc = self.nc
        fin = self._waits("sp", [(k, v, self.epoch) for k, v in self.dma_tokens.items()])
        self.dma_tokens = {}
        q = self.q
        sems = self.sems
        for e in self.ENG:
            c = self.cnt[e]
            for rec in q[e]:
                if rec["kind"] == "op" and rec["signal"]:
                    c += 1
                rec["val"] = c
            self.cnt[e] = c
        self.nsignal = getattr(self, "nsignal", 0) + sum(1 for e in self.ENG for r in q[e] if r["signal"])

        def emit_waits(e, waits):
            for k, v in waits:
                if k in self.cnt:
                    e.wait_ge(sems[k], q[k][v]["val"])
                else:
                    e.wait_ge(sems[k], v)

        def replay(ename, e):
            sem = sems[ename]
            for rec in q[ename]:
                emit_waits(e, rec["waits"])
                if rec["kind"] == "op":
                    ins = rec["fn"](e)
                    if rec["signal"]:
                        ins.then_inc(sem, 1)
                else:
                    e.dma_start(out=rec["out"], in_=rec["in_"], **rec["kw"]).then_inc(sems[rec["key"]], 16)
            if ename == "sp":
                emit_waits(e, fin)

        with nc.Block() as block:
            @block.tensor
            def _(e):
                replay("pe", e)

            @block.scalar
            def _(e):
                replay("act", e)

            @block.vector
            def _(e):
                replay("dve", e)

            @block.gpsimd
            def _(e):
                replay("pool", e)

            @block.sync
            def _(e):
                replay("sp", e)
        self.q = {e: [] for e in self.ENG}
        self.epoch += 1
        for e in self.ENG:
            self.known[e] = dict(self.dma_all)


class Phase:
    def __init__(self, P, name):
        self.P = P
        self.name = name
        self.es = ExitStack()
        self.bufs = []
        self.n = 0

    def sb(self, shape, dt, name=None):
        self.n += 1
        nm = "%s_%s%d" % (self.name, name or "s", self.n)
        b = Buf(nm, self.es.enter_context(self.P.nc.sbuf_tensor(nm, shape, dt)))
        self.bufs.append(b)
        return b

    def ps(self, shape, dt, name=None):
        self.n += 1
        nm = "%s_%s%d" % (self.name, name or "p", self.n)
        per = 512 if dt == F32 else 1024
        n = 1
        for d in shape[1:]:
            n *= d
        padded = -(-n // per) * per
        flat = self.es.enter_context(self.P.nc.psum_tensor(nm, [128, padded], dt))
        v = flat[:, 0:n]
        if len(shape) == 3:
            v = v.rearrange("p (a b) -> p a b", a=shape[1])
        elif len(shape) == 4:
            v = v.rearrange("p (a b c) -> p a b c", a=shape[1], b=shape[2])
        b = Buf(nm, v, psum=True)
        self.bufs.append(b)
        return b

    def close(self):
        self.P.S.flush()
        self.P.S.release(self.bufs)
        self.es.close()


def bc(ap, shape):
    return ap.broadcast_to(shape)


class Prog:
    def __init__(self, seqs, depth, nlayers_total=4, phases=("A", "H", "B", "C", "O", "F"), debug=False):
        self.debug = debug
        self.seqs = seqs
        self.depth = depth
        self.LT = nlayers_total
        self.phases = phases
        self.nc = bass.Bass("TRN2", target_bir_lowering=False)
        nc = self.nc
        self.din = {}

        def inp(name, shape):
            self.din[name] = nc.dram_tensor(name, shape, F32, kind="ExternalInput").ap()
            return self.din[name]

        LT = self.LT
        self.x_in = [inp("x%d" % i, [T, D]) for i, T in enumerate(seqs)]
        self.y_out = [nc.dram_tensor("y%d" % i, [T, D], F32, kind="ExternalOutput").ap()
                      for i, T in enumerate(seqs)]
        inp("norm1_g", [LT, D]); inp("w_in", [LT, D, INW]); inp("lb_raw", [LT, 2, 256])
        inp("a_norm_g", [LT, 64]); inp("b_qn_g", [LT, 64]); inp("b_kn_g", [LT, 64])
        inp("b_sink", [LT, 6]); inp("c_qn_g", [LT, 64]); inp("c_kn_g", [LT, 64])
        inp("w_out", [LT, 768, D]); inp("norm2_g", [LT, D])
        inp("w_gate", [LT, D, DFF]); inp("w_up", [LT, D, DFF]); inp("w_down", [LT, DFF, D])
        self.Tmax = max(seqs)
        inp("c_ident", [128, 128]); inp("c_rope", [self.Tmax, 128]); inp("c_mab", [128, 256])
        inp("c_mh", [128, 256]); inp("c_L", [128, 6 * 128]); inp("c_sel", [128, 2])

        def scr(name, shape, dt):
            return nc.dram_tensor(name, shape, dt, kind="ExternalOutput" if self.debug else "Internal").ap()

        self.PA = [scr("PA%d" % i, [T, OTW], BF16) for i, T in enumerate(seqs)]
        self.Z = [scr("Z%d" % i, [T // 128, 128, 8], F32) for i, T in enumerate(seqs)]
        self.OFB = [scr("OFB%d" % i, [2, T, 256], F32) for i, T in enumerate(seqs)]
        self.YB = [scr("YB%d" % i, [T, 384], BF16) for i, T in enumerate(seqs)]
        self.NZ = [scr("NZ%d" % i, [3, T, 130], F32) for i, T in enumerate(seqs)]
        self.H = [scr("H%d" % i, [T, D], F32) for i, T in enumerate(seqs)]
        self.X = [scr("X%d" % i, [T, D], F32) for i, T in enumerate(seqs)]

    def build(self):
        nc = self.nc
        with ExitStack() as es:
            self.S = Sched(nc, es)
            self.glob = Phase(self, "g")
            self.setup()
            for l in range(self.depth):
                xin = self.x_in if l == 0 else self.X
                xout = self.y_out if l == self.depth - 1 else self.X
                if "A" in self.phases:
                    self.phase_A(l, xin)
                if "H" in self.phases:
                    self.phase_H(l)
                if "B" in self.phases:
                    self.phase_B(l)
                if "C" in self.phases:
                    self.phase_C(l)
                if "O" in self.phases:
                    self.phase_O(l, xin)
                if "F" in self.phases:
                    self.phase_F(l, xout)
            self.S.flush()
            self.glob.es.close()
        return nc

    def setup(self):
        S, G, din = self.S, self.glob, self.din
        LT = self.LT
        self.idb = G.sb([128, 128], BF16, "idb")
        self.mab = G.sb([128, 256], BF16, "mab")
        self.mh = G.sb([128, 256], BF16, "mh")
        self.Lm = G.sb([128, 768], F32, "Lm")
        self.sel = G.sb([128, 2], F32, "sel")
        self.lb = G.sb([128, LT, 512], F32, "lb")
        self.omlb = G.sb([128, LT, 512], F32, "omlb")
        self.epsb = G.sb([128, 1], F32, "epsb")
        self.oneb = G.sb([128, 1], F32, "oneb")
        ph = Phase(self, "su")
        stg = ph.sb([128, 768], F32)
        S.op("dve", lambda e: e.memset(self.epsb[:], EPS), [], [self.epsb])
        S.op("dve", lambda e: e.memset(self.oneb[:], 1.0), [], [self.oneb])
        S.dma("sp", stg[:, 0:128], din["c_ident"], stg, True)
        S.dma("sp", stg[:, 128:384], din["c_mab"], stg, True)
        S.dma("sp", stg[:, 384:640], din["c_mh"], stg, True)
        S.dma("sp", self.Lm[:], din["c_L"], self.Lm, True)
        S.dma("sp", self.sel[:], din["c_sel"], self.sel, True)
        S.op("dve", lambda e: e.tensor_copy(out=self.idb[:], in_=stg[:, 0:128]), [stg], [self.idb])
        S.op("dve", lambda e: e.tensor_copy(out=self.mab[:], in_=stg[:, 128:384]), [stg], [self.mab])
        S.op("dve", lambda e: e.tensor_copy(out=self.mh[:], in_=stg[:, 384:640]), [stg], [self.mh])
        raw = ph.sb([128, LT, 512], F32)
        ex = ph.sb([128, LT, 512], F32)
        ssum = ph.sb([128, 512], F32)
        src = din["lb_raw"].rearrange("l d c -> (l d c)").unsqueeze(0).broadcast_to([128, LT * 512])
        S.dma("sp", raw[:].rearrange("p l c -> p (l c)"), src, raw, True)
        S.op("act", lambda e: e.activation(out=ex[:], in_=raw[:], func=AF.Exp), [raw], [ex])
        S.op("dve", lambda e: e.tensor_tensor(out=ssum[:], in0=ex[:, 0, :], in1=ex[:, 1, :], op=ALU.add), [ex], [ssum])
        for l in range(2, LT):
            S.op("dve", lambda e, l=l: e.tensor_tensor(out=ssum[:], in0=ssum[:], in1=ex[:, l, :], op=ALU.add), [ex, ssum], [ssum])
        S.op("dve", lambda e: e.reciprocal(out=ssum[:], in_=ssum[:]), [ssum], [ssum])
        S.op("dve", lambda e: e.memset(self.lb[:, 0, :], 0.0), [], [self.lb])
        for l in range(1, LT):
            S.op("dve", lambda e, l=l: e.tensor_tensor(out=ex[:, l, :], in0=ex[:, l, :], in1=ssum[:], op=ALU.mult), [ex, ssum], [ex])
            S.op("dve", lambda e, l=l: e.tensor_tensor(out=self.lb[:, l, :], in0=self.lb[:, l - 1, :], in1=ex[:, l, :], op=ALU.add), [ex, self.lb], [self.lb])
        S.op("dve", lambda e: e.tensor_scalar(out=self.omlb[:], in0=self.lb[:], scalar1=-1.0, scalar2=1.0, op0=ALU.mult, op1=ALU.add), [self.lb], [self.omlb])
        ph.close()

    def rstd_from_ss(self, ss, rstd, n):
        S = self.S
        S.op("act", lambda e: e.activation(out=rstd[:], in_=ss[:], func=AF.Sqrt, scale=1.0 / n, bias=self.epsb[:]), [ss, self.epsb], [rstd])
        S.op("dve", lambda e: e.reciprocal(out=rstd[:], in_=rstd[:]), [rstd], [rstd])

    def rstd_lnexp(self, ss, rstd, n):
        S = self.S
        S.op("act", lambda e: e.activation(out=rstd[:], in_=ss[:], func=AF.Ln, scale=1.0 / n, bias=self.epsb[:]), [ss, self.epsb], [rstd])
        S.op("act", lambda e: e.activation(out=rstd[:], in_=rstd[:], func=AF.Exp, scale=-0.5), [rstd], [rstd])

    def load_weight(self, ph, dst, src_rows, gvec, stage, segs, width):
        S = self.S
        KC = dst.t.shape[1]
        for kc in range(KC):
            st = stage[kc % len(stage)]
            S.dma("sp", st[:, 0:width], src_rows(kc), st, True)
            for i, (dc, sc, n) in enumerate(segs):
                eng = "dve" if (i + kc) % 2 == 0 else "act"
                if gvec is not None and eng == "act":
                    S.op(eng, lambda e, kc=kc, dc=dc, sc=sc, n=n, st=st: e.activation(
                        out=dst[:, kc, dc:dc + n], in_=st[:, sc:sc + n], func=AF.Copy, scale=gvec[:, kc:kc + 1]), [st, gvec], [dst])
                elif gvec is not None:
                    S.op(eng, lambda e, kc=kc, dc=dc, sc=sc, n=n, st=st: e.tensor_scalar(
                        out=dst[:, kc, dc:dc + n], in0=st[:, sc:sc + n], scalar1=gvec[:, kc:kc + 1], scalar2=None, op0=ALU.mult),
                        [st, gvec], [dst])
                elif eng == "act":
                    S.op(eng, lambda e, kc=kc, dc=dc, sc=sc, n=n, st=st: e.activation(
                        out=dst[:, kc, dc:dc + n], in_=st[:, sc:sc + n], func=AF.Copy), [st], [dst])
                else:
                    S.op(eng, lambda e, kc=kc, dc=dc, sc=sc, n=n, st=st: e.tensor_copy(
                        out=dst[:, kc, dc:dc + n], in_=st[:, sc:sc + n]), [st], [dst])

    def load_weight_simple(self, dst, src, gvec, stage, width, piece=1024):
        S = self.S
        KC = dst.t.shape[1]
        n = 0
        for kc in range(KC):
            for c0 in range(0, width, piece):
                w = min(piece, width - c0)
                ent = stage[n % len(stage)]
                st, sap = ent if isinstance(ent, tuple) else (ent, ent[:, :])
                S.dma("sp", sap[:, 0:w], src[kc * 128:(kc + 1) * 128, c0:c0 + w], st, True)
                eng = "dve" if n % 2 == 0 else "act"
                if eng == "act":
                    if gvec is not None:
                        S.op(eng, lambda e, kc=kc, c0=c0, w=w, sap=sap: e.activation(
                            out=dst[:, kc, c0:c0 + w], in_=sap[:, 0:w], func=AF.Copy, scale=gvec[:, kc:kc + 1]), [st, gvec], [dst])
                    else:
                        S.op(eng, lambda e, kc=kc, c0=c0, w=w, sap=sap: e.activation(out=dst[:, kc, c0:c0 + w], in_=sap[:, 0:w], func=AF.Copy), [st], [dst])
                elif gvec is not None:
                    S.op(eng, lambda e, kc=kc, c0=c0, w=w, sap=sap: e.tensor_scalar(
                        out=dst[:, kc, c0:c0 + w], in0=sap[:, 0:w], scalar1=gvec[:, kc:kc + 1], scalar2=None, op0=ALU.mult), [st, gvec], [dst])
                else:
                    S.op(eng, lambda e, kc=kc, c0=c0, w=w, sap=sap: e.tensor_copy(out=dst[:, kc, c0:c0 + w], in_=sap[:, 0:w]), [st], [dst])
                n += 1

    def phase_A(self, l, xin):
        S, din = self.S, self.din
        ph = Phase(self, "A%d" % l)
        wi = ph.sb([128, 8, INW], BF16, "wi")
        stage = [ph.sb([128, INW], F32, "stg") for _ in range(2)]
        g1 = ph.sb([128, 8], F32, "g1")
        S.dma("sp", g1[:], din["norm1_g"][l].rearrange("(c p) -> p c", p=128), g1, True, allow_slow_non_contiguous=True)
        segs = [(0, 256, 512), (512, 0, 256), (768, 768, 256), (1024, 1024, 256)]
        dc = 1280
        for h in (0, 3, 1, 4, 2, 5):
            segs.append((dc, 1280 + 64 * h, 64)); dc += 64
        segs.append((dc, 1664, 128)); dc += 128
        segs.append((dc, 1920, 384)); dc += 384
        segs.append((dc, 2304, 384)); dc += 384
        segs.append((dc, 1792, 128)); dc += 128
        segs.append((dc, 2688, 384)); dc += 384
        assert dc == INW
        self.load_weight(ph, wi, lambda kc: din["w_in"][l, kc * 128:(kc + 1) * 128, :], g1, stage, segs, INW)
        Gqk = ph.sb([128, 20, 64], F32, "Gqk")
        gs = ph.sb([128, 4, 64], F32, "gs")
        for j, nm in enumerate(("b_qn_g", "b_kn_g", "c_qn_g", "c_kn_g")):
            S.dma("sp", gs[:, j, :], din[nm][l:l + 1, :].broadcast_to([128, 64]), gs, True)
        for (h0, n, j, sc) in ((0, 6, 0, 0.125), (6, 2, 1, 1.0), (8, 6, 2, 0.125), (14, 6, 3, 1.0)):
            S.op("dve", lambda e, h0=h0, n=n, j=j, sc=sc: e.tensor_scalar(
                out=Gqk[:, h0:h0 + n, :], in0=gs[:, j, :].unsqueeze(1).broadcast_to([128, n, 64]),
                scalar1=sc, scalar2=None, op0=ALU.mult), [gs], [Gqk])
        lbv = self.lb[:, l, :]
        omv = self.omlb[:, l, :]

        xt = [ph.sb([128, D], F32, "xt") for _ in range(2)]
        cs = [ph.sb([128, 128], F32, "cs") for _ in range(3)]
        junk = ph.sb([128, D], BF16, "junk")
        ss = ph.sb([128, 1], F32); rstd = ph.sb([128, 1], F32)
        xb = [ph.sb([128, D], BF16, "xb") for _ in range(2)]
        xT = [ph.sb([128, 8, 128], BF16, "xT") for _ in range(2)]
        pT = ph.ps([128, 8, 128], BF16, "pT")
        pj = [ph.ps([128, 512], F32, "pj") for _ in range(3)]
        pc = ph.ps([128, 6, 256], F32, "pc")
        pz = ph.ps([128, 8], F32, "pz")
        sg = ph.sb([128, 512], F32, "sg"); t1 = ph.sb([128, 512], F32, "t1")
        ff = ph.sb([128, 512], F32, "ff"); kk2b = [ph.sb([128, 512], F32, "kk") for _ in range(2)]
        gl = ph.sb([128, 512], F32, "gl")
        ghi = ph.sb([128, 512], BF16, "ghi"); glo = ph.sb([128, 512], BF16, "glo")
        Lmb = ph.sb([128, 768], BF16, "Lmb"); selb = ph.sb([128, 2], BF16, "selb")
        S.op("dve", lambda e: e.tensor_copy(out=Lmb[:], in_=self.Lm[:]), [self.Lm], [Lmb])
        S.op("dve", lambda e: e.tensor_copy(out=selb[:], in_=self.sel[:]), [self.sel], [selb])
        qi2b = [ph.sb([128, 256], F32, "qi") for _ in range(2)]
        sgl = ph.sb([128, 256], F32, "sgl")
        pq2b = [ph.sb([128, 1536], F32, "pq") for _ in range(2)]
        sq = ph.sb([128, 20, 64], F32, "sq")
        qn = ph.sb([128, 20, 64], F32, "qn")
        ra = ph.sb([128, 20, 64], F32, "ra"); rb = ph.sb([128, 20, 64], F32, "rb")
        ssq = ph.sb([128, 20], F32); rs2b = [ph.sb([128, 20], F32) for _ in range(2)]
        Epos = ph.sb([128, 6, 256], F32, "Epos"); Eneg = ph.sb([128, 2, 256], F32, "Eneg")
        zt = [ph.sb([128, 8], F32, "zt") for _ in range(2)]
        ot = [ph.sb([128, OTW], BF16, "ot") for _ in range(2)]
        Lm = self.Lm

        tiles = [(si, i) for si, T in enumerate(self.seqs) for i in range(T // 128)]

        def norm(n):
            si, i = tiles[n]
            x = xt[n % 2]
            xbn = xb[n % 2]
            S.dma("sp", x[:], xin[si][i * 128:(i + 1) * 128, :], x, True)
            S.dma("sp", cs[n % 3][:], din["c_rope"][i * 128:(i + 1) * 128, :], cs[n % 3], True)
            S.op("act", lambda e: e.activation(out=junk[:], in_=x[:], func=AF.Square, accum_out=ss[:]), [x], [junk, ss])
            self.rstd_lnexp(ss, rstd, D)
            S.op("dve", lambda e: e.tensor_scalar(out=xbn[:], in0=x[:], scalar1=rstd[:], scalar2=None, op0=ALU.mult), [x, rstd], [xbn])

        def trans(n):
            xbn = xb[n % 2]
            for c in range(8):
                S.op("pe", lambda e, c=c: e.transpose(out=pT[:, c, :], in_=xbn[:, c * 128:(c + 1) * 128], identity=self.idb[:]), [xbn, self.idb], [pT])
            xTn = xT[n % 2]
            S.op("act", lambda e: e.activation(out=xTn[:], in_=pT[:], func=AF.Copy), [pT], [xTn])

        def main(n):
            si, i = tiles[n]
            xTn = xT[n % 2]
            o = ot[n % 2]
            c_ = cs[n % 3]
            z = zt[n % 2]
            kk, qi, pq, rs = kk2b[n % 2], qi2b[n % 2], pq2b[n % 2], rs2b[n % 2]
            prev = pending.pop(n - 1, [[] for _ in range(6)])
            for ch in range(6):
                p = pj[ch % 3]
                for kc in range(8):
                    S.op("pe", lambda e, p=p, kc=kc, ch=ch: e.matmul(p[:], lhsT=xTn[:, kc, :], rhs=wi[:, kc, ch * 512:(ch + 1) * 512], start=(kc == 0), stop=(kc == 7)), [xTn, wi], [p])
                if ch == 0:
                    S.op("act", lambda e, p=p: e.activation(out=sg[:], in_=p[:], func=AF.Exp, scale=-1.0), [p], [sg])
                elif ch == 1:
                    S.op("act", lambda e, p=p: e.activation(out=qi[:], in_=p[:, 0:256], func=AF.Copy), [p], [qi])
                    S.op("act", lambda e, p=p: e.activation(out=o[:, O_V:O_V + 256], in_=p[:, 256:512], func=AF.Copy), [p], [o])
                    S.op("act", lambda e: e.activation(out=sg[:], in_=sg[:], func=AF.Ln, bias=self.oneb[:]), [sg, self.oneb], [sg])
                    S.op("act", lambda e: e.activation(out=sg[:], in_=sg[:], func=AF.Exp, scale=-1.0), [sg], [sg])
                    S.op("dve", lambda e: e.tensor_tensor(out=t1[:], in0=sg[:], in1=omv, op=ALU.mult), [sg, self.omlb], [t1])
                    S.op("dve", lambda e: e.scalar_tensor_tensor(out=ff[:], in0=t1[:], scalar=FLOOR, in1=lbv, op0=ALU.max, op1=ALU.add), [t1, self.lb], [ff])
                    S.op("dve", lambda e: e.tensor_tensor(out=kk[:], in0=omv, in1=t1[:], op=ALU.subtract), [t1, self.omlb], [kk])
                elif ch in (2, 3, 4):
                    S.op("act", lambda e, p=p, ch=ch: e.activation(out=pq[:, (ch - 2) * 512:(ch - 1) * 512], in_=p[:], func=AF.Copy), [p], [pq])
                    if ch == 3:
                        S.op("act", lambda e: e.activation(out=gl[:], in_=ff[:], func=AF.Ln), [ff], [gl])
                        S.op("act", lambda e: e.activation(out=ghi[:], in_=gl[:], func=AF.Copy), [gl], [ghi])
                        S.op("dve", lambda e: e.tensor_tensor(out=glo[:], in0=gl[:], in1=ghi[:], op=ALU.subtract), [gl, ghi], [glo])
                    if ch == 4:
                        S.op("act", lambda e: e.activation(out=sq[:], in_=pq[:, 256:1536].rearrange("p (h c) -> p h c", h=20), func=AF.Square), [pq], [sq])
                        S.op("dve", lambda e: e.tensor_reduce(out=ssq[:], in_=sq[:], axis=AX.X, op=ALU.add), [sq], [ssq])
                else:
                    S.op("act", lambda e, p=p: e.activation(out=o[:, O_BV:O_BV + 128], in_=p[:, 0:128], func=AF.Copy), [p], [o])
                    S.op("act", lambda e, p=p: e.activation(
                        out=o[:, O_C:O_C + 1152].rearrange("p (g c) -> p g c", g=3)[:, :, 256:384],
                        in_=p[:, 128:512].rearrange("p (g c) -> p g c", g=3), func=AF.Copy), [p], [o])
                    self.rstd_lnexp(ssq, rs, 64)
                for st in prev[ch]:
                    st()
            for j in range(6):
                d = j // 3
                for part, gsrc in enumerate((ghi, glo)):
                    S.op("pe", lambda e, j=j, d=d, part=part, gsrc=gsrc: e.matmul(pc[:, j, :], lhsT=Lmb[:, j * 128:(j + 1) * 128], rhs=gsrc[:, d * 256:(d + 1) * 256], start=(part == 0), stop=(part == 1)), [Lmb, gsrc], [pc])
            for d in range(2):
                for hp in range(2):
                    idx = d * 2 + hp
                    for part, gsrc in enumerate((ghi, glo)):
                        S.op("pe", lambda e, d=d, hp=hp, idx=idx, part=part, gsrc=gsrc: e.matmul(pz[:, idx * 2:idx * 2 + 2], lhsT=gsrc[:, d * 256 + hp * 128:d * 256 + hp * 128 + 128], rhs=selb[:], start=(part == 0), stop=(part == 1)), [gsrc, selb], [pz])
            if n + 1 < nt:
                trans(n + 1)
            if n + 2 < nt:
                norm(n + 2)
            od = o[:, 0:2048].rearrange("p (d c) -> p d c", d=2)
            q2 = qi[:, :].unsqueeze(1).broadcast_to([128, 2, 256])
            kk2 = kk[:].rearrange("p (d c) -> p d c", d=2)
            pq2 = pq[:, 256:1536].rearrange("p (h c) -> p h c", h=20)
            cc = c_[:, 0:64].unsqueeze(1).broadcast_to([128, 20, 64])
            msin = c_[:, 64:96].unsqueeze(1).broadcast_to([128, 20, 32])
            psin = c_[:, 96:128].unsqueeze(1).broadcast_to([128, 20, 32])
            oc = o[:, O_C:O_C + 1152].rearrange("p (g c) -> p g c", g=3)
            P0 = [
                lambda: S.op("act", lambda e: e.activation(out=Epos[:], in_=pc[:], func=AF.Exp), [pc], [Epos]),
                lambda: S.op("act", lambda e: e.activation(out=Eneg[:], in_=pc[:, 0:6:3, :], func=AF.Exp, scale=-1.0), [pc], [Eneg]),
                lambda: S.op("act", lambda e: e.activation(out=z[:], in_=pz[:], func=AF.Exp), [pz], [z]),
                lambda: S.dma("pool", self.Z[si][i], z[:], z, False),
                lambda: S.op("act", lambda e: e.activation(out=sgl[:], in_=pq[:, 0:256], func=AF.Exp, scale=-1.0), [pq], [sgl]),
                lambda: S.op("act", lambda e: e.activation(out=sgl[:], in_=sgl[:], func=AF.Ln, bias=self.oneb[:]), [sgl, self.oneb], [sgl]),
                lambda: S.op("act", lambda e: e.activation(out=sgl[:], in_=sgl[:], func=AF.Exp, scale=-1.0), [sgl], [sgl]),
            ]
            P1 = [
                lambda: S.op("dve", lambda e: e.tensor_tensor(out=od[:, :, 0:256], in0=q2, in1=Epos[:, 0:6:3, :], op=ALU.mult), [qi, Epos], [o]),
                lambda: S.op("dve", lambda e: e.tensor_tensor(out=od[:, :, 256:512], in0=kk2, in1=Eneg[:], op=ALU.mult), [kk, Eneg], [o]),
                lambda: S.op("dve", lambda e: e.tensor_tensor(out=od[:, :, 512:768], in0=q2, in1=Epos[:, 1:6:3, :], op=ALU.mult), [qi, Epos], [o]),
                lambda: S.op("dve", lambda e: e.tensor_tensor(out=od[:, :, 768:1024], in0=kk2, in1=Epos[:, 2:6:3, :], op=ALU.mult), [kk, Epos], [o]),
                lambda: S.op("dve", lambda e: e.tensor_tensor(out=o[:, O_GATE:O_GATE + 256], in0=pq[:, 0:256], in1=sgl[:], op=ALU.mult), [pq, sgl], [o]),
            ]
            P2 = [
                lambda: S.op("dve", lambda e: e.tensor_tensor(out=qn[:], in0=pq2, in1=rs[:, :].unsqueeze(2).broadcast_to([128, 20, 64]), op=ALU.mult), [pq, rs], [qn]),
                lambda: S.op("dve", lambda e: e.tensor_tensor(out=qn[:], in0=qn[:], in1=Gqk[:], op=ALU.mult), [qn, Gqk], [qn]),
            ]
            P3 = [
                lambda: S.op("dve", lambda e: e.tensor_tensor(out=ra[:], in0=qn[:], in1=cc, op=ALU.mult), [qn, c_], [ra]),
                lambda: S.op("dve", lambda e: e.tensor_tensor(out=rb[:, :, 0:32], in0=qn[:, :, 32:64], in1=msin, op=ALU.mult), [qn, c_], [rb]),
                lambda: S.op("dve", lambda e: e.tensor_tensor(out=rb[:, :, 32:64], in0=qn[:, :, 0:32], in1=psin, op=ALU.mult), [qn, c_], [rb]),
            ]
            P4 = [
                lambda: S.op("dve", lambda e: e.tensor_tensor(out=o[:, O_BQ:O_BQ + 512].rearrange("p (h c) -> p h c", h=8), in0=ra[:, 0:8, :], in1=rb[:, 0:8, :], op=ALU.add), [ra, rb], [o]),
                lambda: S.op("dve", lambda e: e.tensor_tensor(out=oc[:, :, 0:128], in0=ra[:, 8:14, :].rearrange("p (g h) c -> p g (h c)", g=3),
                                                              in1=rb[:, 8:14, :].rearrange("p (g h) c -> p g (h c)", g=3), op=ALU.add), [ra, rb], [o]),
                lambda: S.op("dve", lambda e: e.tensor_tensor(out=oc[:, :, 128:256], in0=ra[:, 14:20, :].rearrange("p (g h) c -> p g (h c)", g=3),
                                                              in1=rb[:, 14:20, :].rearrange("p (g h) c -> p g (h c)", g=3), op=ALU.add), [ra, rb], [o]),
                lambda: S.dma("pool", self.PA[si][i * 128:(i + 1) * 128, :], o[:], o, False),
            ]
            pending[n] = [P0, P1, P2, P3, P4, []]

        pending = {}
        nt = len(tiles)
        norm(0)
        trans(0)
        if nt > 1:
            norm(1)
        for n in range(nt):
            main(n)
        for grp in pending.pop(nt - 1):
            for st in grp:
                st()
        ph.close()

    def phase_H(self, l):
        S = self.S
        ph = Phase(self, "H%d" % l)
        NL = 3
        hin = [[ph.sb([128, 1280], BF16, "hin") for _ in range(NL)] for d in range(2)]
        vz = [[[ph.sb([128, 256], BF16, "vz") for j in range(2)] for _ in range(NL)] for d in range(2)]
        zt = [[ph.sb([128, 4], F32, "zt") for _ in range(NL)] for d in range(2)]
        hT = [[ph.sb([128, 6, 128], BF16, "hT") for _ in range(2)] for d in range(2)]
        Am = [[ph.sb([128, 4, 128], BF16, "Am") for _ in range(2)] for d in range(2)]
        St = [ph.sb([128, 2, 64], F32, "St") for d in range(2)]
        Sb = [[ph.sb([128, 2, 64], BF16, "Sb") for _ in range(2)] for d in range(2)]
        osb = [[ph.sb([128, 4, 64], F32, "osb") for _ in range(2)] for d in range(2)]
        pT = ph.ps([128, 6, 128], BF16, "pT")
        pA = [ph.ps([128, 2, 128], F32, "pA") for r in range(2)]
        pO = [ph.ps([128, 2, 2, 64], F32, "pO") for r in range(2)]
        pS = [ph.ps([128, 2, 64], F32, "pS") for d in range(2)]
        idb, mh = self.idb, self.mh
        for d in range(2):
            for sl in range(NL):
                for j in range(2):
                    S.op("pool", lambda e, b=vz[d][sl][j]: e.memset(b[:], 0.0), [], [vz[d][sl][j]])
        for si, T in enumerate(self.seqs):
            n = T // 128
            sbi = [0, 0]
            for d in range(2):
                S.op("dve", lambda e, d=d: e.memset(St[d][:], 0.0), [], [St[d]])
                S.op("dve", lambda e, d=d: e.memset(Sb[d][0][:], 0.0), [], [Sb[d][0]])

            def load(step):
                for d in range(2):
                    ti = step if d == 0 else n - 1 - step
                    h = hin[d][step % NL]
                    S.dma("sp", h[:, 0:1024], self.PA[si][ti * 128:(ti + 1) * 128, d * 1024:(d + 1) * 1024], h, True)
                    S.dma("sp", h[:, 1024:1280], self.PA[si][ti * 128:(ti + 1) * 128, O_V:O_V + 256], h, True)
                    for j in range(2):
                        v = vz[d][step % NL][j]
                        S.dma("sp", v[j * 64:(j + 1) * 64, :], self.PA[si][ti * 128 + j * 64:ti * 128 + (j + 1) * 64, O_V:O_V + 256], v, True)
                    z = zt[d][step % NL]
                    S.dma("sp", z[:], self.Z[si][ti][:, d * 4:(d + 1) * 4], z, True)

            def prep(step, d):
                h = hin[d][step % NL]
                hTd = hT[d][step % 2]
                Amd = Am[d][step % 2]
                for j in range(3):
                    for hp in range(2):
                        S.op("pe", lambda e, j=j, hp=hp, h=h: e.transpose(out=pT[:, j * 2 + hp, :], in_=h[:, j * 256 + hp * 128: j * 256 + hp * 128 + 128], identity=idb[:]), [h, idb], [pT])
                S.op("act", lambda e: e.activation(out=hTd[:], in_=pT[:], func=AF.Copy), [pT], [hTd])
                for hh in range(4):
                    hp, r = hh // 2, hh % 2
                    kb = r * 64
                    S.op("pe", lambda e, hp=hp, r=r, kb=kb: e.matmul(
                        pA[r][:, hp, :], lhsT=hTd[kb:kb + 64, 2 + hp, :], rhs=hTd[kb:kb + 64, 0 + hp, :], start=True, stop=True), [hTd], [pA[r]])
                for r in range(2):
                    S.op("dve", lambda e, r=r: e.tensor_tensor(
                        out=Amd[:, r:4:2, :], in0=pA[r][:], in1=mh[:, d * 128:(d + 1) * 128].unsqueeze(1).broadcast_to([128, 2, 128]), op=ALU.mult), [pA[r], mh], [Amd])

            def chain(step):
                for jj in range(2):
                    for d in range(2):
                        j = jj if d == 0 else 1 - jj
                        h = hin[d][step % NL]
                        hTd = hT[d][step % 2]
                        Amd = Am[d][step % 2]
                        sbcur = Sb[d][sbi[d] % 2]
                        vzj = vz[d][step % NL][j]
                        for hh in range(4):
                            hp, r = hh // 2, hh % 2
                            kb = r * 64
                            po = pO[r]
                            S.op("pe", lambda e, d=d, j=j, hh=hh, hp=hp, h=h, po=po, Amd=Amd: e.matmul(
                                po[j * 64:(j + 1) * 64, d, hp, :], lhsT=Amd[:, hh, j * 64:(j + 1) * 64],
                                rhs=h[:, 1024 + hh * 64:1024 + (hh + 1) * 64], start=True, stop=False), [Amd, h], [po])
                            S.op("pe", lambda e, d=d, j=j, hp=hp, kb=kb, sbcur=sbcur, po=po, hTd=hTd: e.matmul(
                                po[j * 64:(j + 1) * 64, d, hp, :], lhsT=hTd[kb:kb + 64, 4 + hp, j * 64:(j + 1) * 64],
                                rhs=sbcur[kb:kb + 64, hp, :], start=False, stop=True), [hTd, sbcur], [po])
                        for hh in range(4):
                            hp, r = hh // 2, hh % 2
                            kb = r * 64
                            S.op("pe", lambda e, d=d, hh=hh, hp=hp, kb=kb, h=h, vzj=vzj: e.matmul(
                                pS[d][kb:kb + 64, hp, :], lhsT=h[:, 768 + hh * 64:768 + (hh + 1) * 64],
                                rhs=vzj[:, hh * 64:(hh + 1) * 64], start=True, stop=True), [h, vzj], [pS[d]])
                    for d in range(2):
                        j = jj if d == 0 else 1 - jj
                        z = zt[d][step % NL]
                        sbi[d] += 1
                        sbn = Sb[d][sbi[d] % 2]
                        for hp in range(2):
                            S.op("dve", lambda e, d=d, j=j, hp=hp, z=z, sbn=sbn: e.scalar_tensor_tensor(
                                out=sbn[:, hp, :], in0=St[d][:, hp, :], scalar=z[:, hp * 2 + j:hp * 2 + j + 1], in1=pS[d][:, hp, :],
                                op0=ALU.mult, op1=ALU.add), [St[d], z, pS[d]], [sbn])
                        for hp in range(2):
                            S.op("dve", lambda e, d=d, j=j, hp=hp, z=z: e.scalar_tensor_tensor(
                                out=St[d][:, hp, :], in0=St[d][:, hp, :], scalar=z[:, hp * 2 + j:hp * 2 + j + 1], in1=pS[d][:, hp, :],
                                op0=ALU.mult, op1=ALU.add), [St[d], z, pS[d]], [St[d]])
                for d in range(2):
                    ti = step if d == 0 else n - 1 - step
                    ob = osb[d][step % 2]
                    for r in range(2):
                        S.op("act", lambda e, d=d, r=r, ob=ob: e.activation(out=ob[:, r:4:2, :], in_=pO[r][:, d, :, :], func=AF.Copy), [pO[r]], [ob])
                    S.dma("pool", self.OFB[si][d, ti * 128:(ti + 1) * 128, :], ob[:].rearrange("p h c -> p (h c)"), ob, False)

            load(0)
            if n > 1:
                load(1)
            prep(0, 0)
            prep(0, 1)
            for step in range(n):
                if step + 2 < n:
                    load(step + 2)
                if step + 1 < n:
                    prep(step + 1, 0)
                    prep(step + 1, 1)
                chain(step)
        ph.close()

    def phase_B(self, l):
        S, din = self.S, self.din
        ph = Phase(self, "B%d" % l)
        esk = ph.sb([128, 6], F32, "esk")
        S.dma("sp", esk[:], din["b_sink"][l:l + 1, :].broadcast_to([128, 6]), esk, True)
        S.op("act", lambda e: e.activation(out=esk[:], in_=esk[:], func=AF.Exp), [esk], [esk])
        NS = 5
        kt = [ph.sb([128, 128], BF16, "kt") for _ in range(2)]
        kT = [ph.sb([128, 128], BF16, "kT") for _ in range(NS)]
        v1 = [ph.sb([128, 2, 65], BF16, "v1") for _ in range(NS)]
        qt = [ph.sb([128, 384], BF16, "qt") for _ in range(3)]
        qT = [ph.sb([128, 3, 128], BF16, "qT") for _ in range(2)]
        Pt = [[ph.sb([128, 3, 384], BF16, "Pt") for _ in range(2)] for hk in range(2)]
        den = ph.sb([128, 6], F32, "den")
        yb = [ph.sb([128, 6, 64], BF16, "yb") for _ in range(2)]
        pS = [ph.ps([128, 3, 512], F32, "pS") for _ in range(2)]
        pT = ph.ps([128, 4, 128], BF16, "pT")
        pO = ph.ps([128, 6, 65], F32, "pO")
        idb, mab = self.idb, self.mab
        for b in v1:
            S.op("dve", lambda e, b=b: e.memset(b[:], 1.0), [], [b])
        for si, T in enumerate(self.seqs):
            n = T // 128

            def loadk(m):
                k = kt[m % 2]
                S.dma("sp", k[:], self.PA[si][m * 128:(m + 1) * 128, O_BK:O_BK + 128], k, True)
                S.dma("sp", v1[m % NS][:, :, 0:64], self.PA[si][m * 128:(m + 1) * 128, O_BV:O_BV + 128].rearrange("p (h c) -> p h c", h=2), v1[m % NS], True)
                S.op("pe", lambda e, k=k: e.transpose(out=pT[:, 3, :], in_=k[:], identity=idb[:]), [k, idb], [pT])
                S.op("act", lambda e, m=m: e.activation(out=kT[m % NS][:], in_=pT[:, 3, :], func=AF.Copy), [pT], [kT[m % NS]])

            def loadq(i):
                q = qt[i % 3]
                S.dma("sp", q[:], self.PA[si][i * 128:(i + 1) * 128, O_BQ:O_BQ + 384], q, True)

            def front(i):
                q = qt[i % 3]
                qTi = qT[i % 2]
                for p in range(3):
                    S.op("pe", lambda e, p=p, q=q: e.transpose(out=pT[:, p, :], in_=q[:, p * 128:(p + 1) * 128], identity=idb[:]), [q, idb], [pT])
                S.op("act", lambda e: e.activation(out=qTi[:], in_=pT[:, 0:3, :], func=AF.Copy), [pT], [qTi])
                ms = [m for m in (i - 1, i, i + 1) if 0 <= m < n]
                for hk in range(2):
                    ps_ = pS[hk]
                    P_ = Pt[hk][i % 2]
                    for mi, m in enumerate(ms):
                        S.op("pe", lambda e, hk=hk, mi=mi, m=m, ps_=ps_: e.matmul(
                            ps_[:, mi, 0:384], lhsT=kT[m % NS][hk * 64:(hk + 1) * 64, :],
                            rhs=qTi[hk * 64:(hk + 1) * 64, :, :].rearrange("p a b -> p (a b)"), start=True, stop=True), [kT[m % NS], qTi], [ps_])
                    nm = len(ms)
                    S.op("act", lambda e, ps_=ps_, P_=P_, nm=nm: e.activation(out=P_[:, 0:nm, :], in_=ps_[:, 0:nm, 0:384], func=AF.Exp), [ps_], [P_])
                    for mi, m in enumerate(ms):
                        if m == i:
                            continue
                        mk = mab[:, 0:128] if m < i else mab[:, 128:256]
                        S.op("dve", lambda e, mi=mi, mk=mk, P_=P_: e.tensor_tensor(
                            out=P_[:, mi, :].rearrange("p (h a) -> p h a", h=3), in0=P_[:, mi, :].rearrange("p (h a) -> p h a", h=3),
                            in1=mk.unsqueeze(1).broadcast_to([128, 3, 128]), op=ALU.mult), [P_, mab], [P_])

            def back(i):
                ms = [m for m in (i - 1, i, i + 1) if 0 <= m < n]
                for hk in range(2):
                    P_ = Pt[hk][i % 2]
                    for p in range(3):
                        for mi, m in enumerate(ms):
                            S.op("pe", lambda e, hk=hk, p=p, mi=mi, m=m, P_=P_: e.matmul(
                                pO[:, hk * 3 + p, :], lhsT=P_[:, mi, p * 128:(p + 1) * 128], rhs=v1[m % NS][:, hk, :],
                                start=(mi == 0), stop=(mi == len(ms) - 1)), [P_, v1[m % NS]], [pO])
                S.op("dve", lambda e: e.tensor_tensor(out=den[:], in0=pO[:, :, 64], in1=esk[:], op=ALU.add), [pO, esk], [den])
                S.op("dve", lambda e: e.reciprocal(out=den[:], in_=den[:]), [den], [den])
                y = yb[i % 2]
                S.op("dve", lambda e, y=y: e.tensor_tensor(out=y[:], in0=pO[:, :, 0:64], in1=den[:, :].unsqueeze(2).broadcast_to([128, 6, 64]), op=ALU.mult), [pO, den], [y])
                S.dma("pool", self.YB[si][i * 128:(i + 1) * 128, :], y[:].rearrange("p h c -> p (h c)"), y, False)

            loadk(0); loadq(0)
            if n > 1:
                loadk(1); loadq(1)
            front(0)
            for i in range(n):
                if i + 2 < n:
                    loadk(i + 2); loadq(i + 2)
                if i + 1 < n:
                    front(i + 1)
                back(i)
        ph.close()

    def phase_C(self, l):
        S = self.S
        ph = Phase(self, "C%d" % l)
        NS = 6
        kt = [ph.sb([128, 128], BF16, "kt") for _ in range(2)]
        kT = [ph.sb([128, 128], BF16, "kT") for _ in range(NS)]
        v1 = [ph.sb([128, 2, 65], BF16, "v1") for _ in range(NS)]
        v1e = [[ph.sb([128, 2, 65], BF16, "v1e") for _ in range(3)] for e_ in range(2)]
        qt = [ph.sb([128, 128], BF16, "qt") for _ in range(3)]
        qT = [ph.sb([128, 128], BF16, "qT") for _ in range(2)]
        Pt = [ph.sb([128, 2, 2, 128], BF16, "Pt") for _ in range(2)]
        nz = [ph.sb([128, 130], F32, "nz") for _ in range(2)]
        pT = [ph.ps([128, 2, 128], BF16, "pT") for _ in range(2)]
        pS = [[ph.ps([128, 2, 128], F32, "pS") for hh in range(2)] for _ in range(2)]
        pO = [ph.ps([128, 2, 65], F32, "pO") for _ in range(2)]
        idb, mab = self.idb, self.mab
        for b in v1:
            S.op("dve", lambda e, b=b: e.memset(b[:], 1.0), [], [b])
        for e_ in range(2):
            for b in v1e[e_]:
                S.op("dve", lambda e, b=b: e.memset(b[:], 1.0), [], [b])
                lo = 0 if e_ == 0 else 64
                S.op("dve", lambda e, b=b, lo=lo: e.memset(b[lo:lo + 64, :, :], 0.0), [], [b])
        for b in kt:
            S.op("dve", lambda e, b=b: e.memset(b[:], 0.0), [], [b])
        cnt = [0, 0, 0]
        jobs = []
        for si, T in enumerate(self.seqs):
            for g, dil in enumerate((1, 4, 16)):
                nq = (T // dil) // 128
                for r in range(dil):
                    grp = {"si": si, "g": g, "dil": dil, "nq": nq, "r": r, "kslot": {}, "vbuf": {},
                           "pav": self.PA[si].rearrange("(j d) c -> d j c", d=dil),
                           "nzv": self.NZ[si][g].rearrange("(j d) c -> d j c", d=dil), "c0": O_C + 384 * g}
                    for b in range(nq):
                        jobs.append((grp, b))

        def loadk(grp, m):
            nq, r, pav, c0 = grp["nq"], grp["r"], grp["pav"], grp["c0"]
            lo = 64 if m == 0 else 0
            hi = 64 if m == nq else 128
            j0 = m * 128 - 64 + lo
            j1 = m * 128 - 64 + hi
            ks = cnt[0] % NS
            k = kt[cnt[0] % 2]
            cnt[0] += 1
            grp["kslot"][m] = ks
            if m == 0:
                vb = v1e[0][cnt[1] % 3]; cnt[1] += 1
            elif m == nq:
                vb = v1e[1][cnt[2] % 3]; cnt[2] += 1
            else:
                vb = v1[ks]
            grp["vbuf"][m] = vb
            S.dma("sp", k[lo:hi, :], pav[r, j0:j1, c0 + 128:c0 + 256], k, True)
            S.dma("sp", vb[lo:hi, :, 0:64], pav[r, j0:j1, c0 + 256:c0 + 384].rearrange("p (h c) -> p h c", h=2), vb, True)
            pt_ = pT[cnt[0] % 2]
            S.op("pe", lambda e, k=k, pt_=pt_: e.transpose(out=pt_[:, 1, :], in_=k[:], identity=idb[:]), [k, idb], [pt_])
            S.op("act", lambda e, ks=ks, pt_=pt_: e.activation(out=kT[ks][:], in_=pt_[:, 1, :], func=AF.Copy), [pt_], [kT[ks]])

        def load(k):
            grp, b = jobs[k]
            if b == 0:
                loadk(grp, 0)
            loadk(grp, b + 1)
            q = qt[k % 3]
            S.dma("sp", q[:], grp["pav"][grp["r"], b * 128:(b + 1) * 128, grp["c0"]:grp["c0"] + 128], q, True)

        def front(k):
            grp, b = jobs[k]
            q = qt[k % 3]
            qTi = qT[k % 2]
            pt_ = pT[k % 2]
            P_ = Pt[k % 2]
            S.op("pe", lambda e: e.transpose(out=pt_[:, 0, :], in_=q[:], identity=idb[:]), [q, idb], [pt_])
            S.op("act", lambda e: e.activation(out=qTi[:], in_=pt_[:, 0, :], func=AF.Copy), [pt_], [qTi])
            for hh in range(2):
                ps_ = pS[k % 2][hh]
                for u in range(2):
                    ks = grp["kslot"][b + u]
                    S.op("pe", lambda e, hh=hh, u=u, ks=ks, ps_=ps_: e.matmul(
                        ps_[:, u, :], lhsT=kT[ks][hh * 64:(hh + 1) * 64, :], rhs=qTi[hh * 64:(hh + 1) * 64, :],
                        start=True, stop=True), [kT[ks], qTi], [ps_])
                S.op("act", lambda e, hh=hh, ps_=ps_: e.activation(out=P_[:, hh, :, :], in_=ps_[:], func=AF.Exp), [ps_], [P_])
            S.op("dve", lambda e: e.tensor_tensor(
                out=P_[:].rearrange("p h u a -> p h (u a)"), in0=P_[:].rearrange("p h u a -> p h (u a)"),
                in1=mab[:, :].unsqueeze(1).broadcast_to([128, 2, 256]), op=ALU.mult), [P_, mab], [P_])

        def back(k):
            grp, b = jobs[k]
            P_ = Pt[k % 2]
            po = pO[k % 2]
            for hh in range(2):
                for u in range(2):
                    vb = grp["vbuf"][b + u]
                    S.op("pe", lambda e, hh=hh, u=u, vb=vb: e.matmul(
                        po[:, hh, :], lhsT=P_[:, hh, u, :], rhs=vb[:, hh, :],
                        start=(u == 0), stop=(u == 1)), [P_, vb], [po])
            o = nz[k % 2]
            S.op("act", lambda e: e.activation(out=o[:], in_=po[:].rearrange("p h c -> p (h c)"), func=AF.Copy), [po], [o])
            S.dma("pool", grp["nzv"][grp["r"], b * 128:(b + 1) * 128, :], o[:], o, False)

        nj = len(jobs)
        load(0)
        if nj > 1:
            load(1)
        front(0)
        for k in range(nj):
            if k + 2 < nj:
                load(k + 2)
            if k + 1 < nj:
                front(k + 1)
            back(k)
        ph.close()

    def phase_O(self, l, xin):
        S, din = self.S, self.din
        ph = Phase(self, "O%d" % l)
        NB = 4
        wo = ph.sb([128, 6, D], BF16, "wo")
        stage = [ph.sb([128, D], F32, "stg") for _ in range(2)]
        self.load_weight_simple(wo, din["w_out"][l], None, stage, D)
        Ga = ph.sb([128, 64], F32, "Ga")
        S.dma("sp", Ga[:], din["a_norm_g"][l:l + 1, :].broadcast_to([128, 64]), Ga, True)
        NSL = 3
        xt = [ph.sb([128, NB, D], F32, "xt") for _ in range(NSL)]
        ofb = [ph.sb([128, NB, 2, 256], F32, "ofb") for _ in range(NSL)]
        gt = [ph.sb([128, NB, 256], BF16, "gt") for _ in range(NSL)]
        nz = [ph.sb([128, NB, 3, 130], F32, "nz") for _ in range(NSL)]
        yc = [ph.sb([128, NB, 768], BF16, "yc") for _ in range(NSL)]
        osum = ph.sb([128, NB, 256], F32, "osum"); osq = ph.sb([128, NB, 256], F32, "osq")
        ssq = ph.sb([128, NB * 4], F32); rs = ph.sb([128, NB * 4], F32)
        nsum = ph.sb([128, NB, 130], F32, "nsum"); rden = ph.sb([128, NB, 2], F32)
        ycT = [ph.sb([128, 6, 128], BF16, "ycT") for _ in range(2)]
        ht = [ph.sb([128, NB, D], F32, "ht") for _ in range(2)]
        pT = [ph.ps([128, 6, 128], BF16, "pT") for _ in range(2)]
        pW = [ph.ps([128, 512], F32, "pW") for _ in range(2)]
        idb = self.idb
        blocks = [(si, b) for si, T in enumerate(self.seqs) for b in range(T // (128 * NB))]

        def load(n):
            si, b = blocks[n]
            rows = slice(b * NB * 128, (b + 1) * NB * 128)
            k = n % NSL
            S.dma("sp", xt[k][:], xin[si][rows, :].rearrange("(t p) c -> p t c", p=128), xt[k], True)
            for d in range(2):
                S.dma("sp", ofb[k][:, :, d, :], self.OFB[si][d, rows, :].rearrange("(t p) c -> p t c", p=128), ofb[k], True)
            S.dma("sp", gt[k][:], self.PA[si][rows, O_GATE:O_GATE + 256].rearrange("(t p) c -> p t c", p=128), gt[k], True)
            for g in range(3):
                S.dma("sp", nz[k][:, :, g, :], self.NZ[si][g, rows, :].rearrange("(t p) c -> p t c", p=128), nz[k], True)
            S.dma("sp", yc[k][:, :, 256:640], self.YB[si][rows, :].rearrange("(t p) c -> p t c", p=128), yc[k], True)

        def prep_steps(n):
            k = n % NSL
            of, g_, nz_, y = ofb[k], gt[k], nz[k], yc[k]
            o3 = osum[:].rearrange("p t (h c) -> p (t h) c", h=4)
            n4 = nsum[:].rearrange("p t (h c) -> p t h c", h=2)
            return [
                lambda: S.op("dve", lambda e: e.tensor_tensor(out=osum[:], in0=of[:, :, 0, :], in1=of[:, :, 1, :], op=ALU.add), [of], [osum]),
                lambda: S.op("act", lambda e: e.activation(out=osq[:], in_=osum[:], func=AF.Square), [osum], [osq]),
                lambda: S.op("dve", lambda e: e.tensor_reduce(out=ssq[:], in_=osq[:].rearrange("p t (h c) -> p (t h) c", h=4), axis=AX.X, op=ALU.add), [osq], [ssq]),
                lambda: S.op("dve", lambda e: e.tensor_tensor(out=nsum[:], in0=nz_[:, :, 0, :], in1=nz_[:, :, 1, :], op=ALU.add), [nz_], [nsum]),
                lambda: self.rstd_from_ss(ssq, rs, 64),
                lambda: S.op("dve", lambda e: e.tensor_tensor(out=nsum[:], in0=nsum[:], in1=nz_[:, :, 2, :], op=ALU.add), [nz_, nsum], [nsum]),
                lambda: S.op("dve", lambda e: e.tensor_tensor(out=o3, in0=o3, in1=rs[:, :].unsqueeze(2).broadcast_to([128, NB * 4, 64]), op=ALU.mult), [osum, rs], [osum]),
                lambda: S.op("dve", lambda e: e.reciprocal(out=rden[:], in_=n4[:, :, :, 64]), [nsum], [rden]),
                lambda: S.op("dve", lambda e: e.tensor_tensor(out=o3, in0=o3, in1=Ga[:, :].unsqueeze(1).broadcast_to([128, NB * 4, 64]), op=ALU.mult), [osum, Ga], [osum]),
                lambda: S.op("dve", lambda e: e.tensor_tensor(out=y[:, :, 640:768].rearrange("p t (h c) -> p t h c", h=2), in0=n4[:, :, :, 0:64],
                                                              in1=rden[:, :, :].unsqueeze(3).broadcast_to([128, NB, 2, 64]), op=ALU.mult), [nsum, rden], [y]),
                lambda: S.op("dve", lambda e: e.tensor_tensor(out=y[:, :, 0:256], in0=osum[:], in1=g_[:], op=ALU.mult), [osum, g_], [y]),
            ]

        def prep(n):
            for st in prep_steps(n):
                st()

        def fin(n):
            si, b = blocks[n]
            rows = slice(b * NB * 128, (b + 1) * NB * 128)
            k = n % NSL
            x, y = xt[k], yc[k]
            h = ht[n % 2]
            nxt = prep_steps(n + 1) if n + 1 < nt else []
            per = -(-len(nxt) // NB)

            def tr(t):
                ycTn = ycT[t % 2]
                pt_ = pT[t % 2]
                for c in range(6):
                    S.op("pe", lambda e, c=c, t=t: e.transpose(out=pt_[:, c, :], in_=y[:, t, c * 128:(c + 1) * 128], identity=idb[:]), [y, idb], [pt_])
                S.op("act", lambda e: e.activation(out=ycTn[:], in_=pt_[:], func=AF.Copy), [pt_], [ycTn])

            tr(0)
            for t in range(NB):
                if t + 1 < NB:
                    tr(t + 1)
                ycTn = ycT[t % 2]
                for nc_ in range(2):
                    p = pW[nc_]
                    for kc in range(6):
                        S.op("pe", lambda e, p=p, kc=kc, nc_=nc_, ycTn=ycTn: e.matmul(p[:], lhsT=ycTn[:, kc, :], rhs=wo[:, kc, nc_ * 512:(nc_ + 1) * 512], start=(kc == 0), stop=(kc == 5)), [ycTn, wo], [p])
                    S.op("dve", lambda e, p=p, nc_=nc_, t=t: e.tensor_tensor(out=h[:, t, nc_ * 512:(nc_ + 1) * 512], in0=p[:], in1=x[:, t, nc_ * 512:(nc_ + 1) * 512], op=ALU.add), [p, x], [h])
                for st in nxt[t * per:(t + 1) * per]:
                    st()
            S.dma("pool", self.H[si][rows, :].rearrange("(t p) c -> p t c", p=128), h[:], h, False)

        nt = len(blocks)
        load(0)
        if nt > 1:
            load(1)
        prep(0)
        for n in range(nt):
            if n + 2 < nt:
                load(n + 2)
            fin(n)
        ph.close()

    def phase_F(self, l, xout):
        S, din = self.S, self.din
        ph = Phase(self, "F%d" % l)
        wg = ph.sb([128, 8, DFF], BF16, "wg")
        wu = ph.sb([128, 8, DFF], BF16, "wu")
        wd = ph.sb([128, NFF, D], BF16, "wd")
        stage = [ph.sb([128, 1024], F32, "stg") for _ in range(2)]
        g2 = ph.sb([128, 8], F32, "g2")
        S.dma("sp", g2[:], din["norm2_g"][l].rearrange("(c p) -> p c", p=128), g2, True, allow_slow_non_contiguous=True)
        NB = 2
        ht = [ph.sb([128, NB, D], F32, "ht") for _ in range(2)]
        stage4 = [stage[0], stage[1], (ht[0], ht[0][:, 0, :]), (ht[1], ht[1][:, 0, :])]
        self.load_weight_simple(wg, din["w_gate"][l], g2, stage4, DFF)
        self.load_weight_simple(wu, din["w_up"][l], g2, stage4, DFF)
        self.load_weight_simple(wd, din["w_down"][l], None, stage4, D)
        junk = ph.sb([128, D], BF16, "junk")
        ss = ph.sb([128, 1], F32); rstd = ph.sb([128, 1], F32)
        hb = ph.sb([128, D], BF16, "hb")
        hT = [ph.sb([128, 8, NB * 128], BF16, "hT") for _ in range(2)]
        sgt = [ph.sb([128, NB * 128], F32, "sgt") for _ in range(2)]
        aT = [ph.sb([128, NB * 128], BF16, "aT") for _ in range(2)]
        yo = [ph.sb([128, D], F32, "yo") for _ in range(2)]
        pT = ph.ps([128, 8, 128], BF16, "pT")
        pG = [ph.ps([128, 2, NB * 128], F32, "pG") for _ in range(2)]
        pD = [ph.ps([128, 2, 512], F32, "pD") for _ in range(NB)]
        idb = self.idb
        blocks = [(si, b) for si, T in enumerate(self.seqs) for b in range(T // (128 * NB))]

        def load(n):
            si, b = blocks[n]
            h = ht[n % 2]
            S.dma("sp", h[:], self.H[si][b * NB * 128:(b + 1) * NB * 128, :].rearrange("(t p) c -> p t c", p=128), h, True)

        def front(n):
            h = ht[n % 2]
            hTn = hT[n % 2]
            for t in range(NB):
                S.op("act", lambda e, t=t: e.activation(out=junk[:], in_=h[:, t, :], func=AF.Square, accum_out=ss[:]), [h], [junk, ss])
                self.rstd_from_ss(ss, rstd, D)
                S.op("dve", lambda e, t=t: e.tensor_scalar(out=hb[:], in0=h[:, t, :], scalar1=rstd[:], scalar2=None, op0=ALU.mult), [h, rstd], [hb])
                for c in range(8):
                    S.op("pe", lambda e, c=c: e.transpose(out=pT[:, c, :], in_=hb[:, c * 128:(c + 1) * 128], identity=idb[:]), [hb, idb], [pT])
                S.op("act", lambda e, t=t: e.activation(out=hTn[:, :, t * 128:(t + 1) * 128], in_=pT[:], func=AF.Copy), [pT], [hTn])

        def front_tile(n, t):
            h = ht[n % 2]
            hTn = hT[n % 2]
            S.op("act", lambda e, t=t: e.activation(out=junk[:], in_=h[:, t, :], func=AF.Square, accum_out=ss[:]), [h], [junk, ss])
            self.rstd_from_ss(ss, rstd, D)
            S.op("dve", lambda e, t=t: e.tensor_scalar(out=hb[:], in0=h[:, t, :], scalar1=rstd[:], scalar2=None, op0=ALU.mult), [h, rstd], [hb])
            for c in range(8):
                S.op("pe", lambda e, c=c: e.transpose(out=pT[:, c, :], in_=hb[:, c * 128:(c + 1) * 128], identity=idb[:]), [hb, idb], [pT])
            S.op("act", lambda e, t=t: e.activation(out=hTn[:, :, t * 128:(t + 1) * 128], in_=pT[:], func=AF.Copy), [pT], [hTn])

        def ffn(n):
            si, b = blocks[n]
            h = ht[n % 2]
            hTn = hT[n % 2]

            def gu(f):
                pg = pG[f % 2]
                for j, w in enumerate((wg, wu)):
                    for kc in range(8):
                        S.op("pe", lambda e, j=j, w=w, kc=kc, f=f, pg=pg: e.matmul(pg[:, j, :], lhsT=w[:, kc, f * 128:(f + 1) * 128], rhs=hTn[:, kc, :], start=(kc == 0), stop=(kc == 7)), [w, hTn], [pg])
                sg_ = sgt[f % 2]
                a_ = aT[f % 2]
                S.op("act", lambda e, pg=pg, sg_=sg_: e.activation(out=sg_[:], in_=pg[:, 0, :], func=AF.Silu), [pg], [sg_])
                S.op("dve", lambda e, pg=pg, sg_=sg_, a_=a_: e.tensor_tensor(out=a_[:], in0=pg[:, 1, :], in1=sg_[:], op=ALU.mult), [pg, sg_], [a_])

            def dn(f):
                a_ = aT[f % 2]
                for t in range(NB):
                    for nc_ in range(2):
                        S.op("pe", lambda e, t=t, nc_=nc_, f=f, a_=a_: e.matmul(pD[t][:, nc_, :], lhsT=a_[:, t * 128:(t + 1) * 128], rhs=wd[:, f, nc_ * 512:(nc_ + 1) * 512], start=(f == 0), stop=(f == NFF - 1)), [a_, wd], [pD[t]])

            gu(0)
            for f in range(1, NFF):
                gu(f)
                dn(f - 1)
                if n + 1 < nb:
                    if f == 8:
                        front_tile(n + 1, 0)
                    elif f == 15:
                        front_tile(n + 1, 1)
            dn(NFF - 1)
            for t in range(NB):
                y = yo[t % 2]
                S.op("dve", lambda e, t=t, y=y: e.tensor_tensor(out=y[:], in0=pD[t][:].rearrange("p a b -> p (a b)"), in1=h[:, t, :], op=ALU.add), [pD[t], h], [y])
                r0 = (b * NB + t) * 128
                S.dma("pool", xout[si][r0:r0 + 128, :], y[:], y, False)

        nb = len(blocks)
        load(0)
        front(0)
        for n in range(nb):
            if n + 1 < nb:
                load(n + 1)
            ffn(n)
        ph.close()


def make_consts(Tmax):
    c = {}
    c["c_ident"] = np.eye(128, dtype=np.float32)
    pos = np.arange(Tmax, dtype=np.float32)
    inv = (10000.0 ** (-np.arange(32, dtype=np.float32) / 32)).astype(np.float32)
    ang = pos[:, None] * inv[None, :]
    cos = np.cos(ang).astype(np.float32); sin = np.sin(ang).astype(np.float32)
    c["c_rope"] = np.concatenate([cos, cos, -sin, sin], axis=1).astype(np.float32)
    ci = np.arange(128)[:, None]; ai = np.arange(128)[None, :]
    c["c_mab"] = np.concatenate([(ci >= ai), (ci <= ai)], axis=1).astype(np.float32)
    s = np.arange(128)[:, None]; t = np.arange(128)[None, :]
    same = (s // 64) == (t // 64)
    sL = s % 64; tL = t % 64
    c["c_mh"] = np.concatenate([same & (sL <= tL), same & (sL >= tL)], axis=1).astype(np.float32)
    Ls = [
        same * ((sL <= tL).astype(np.float32) - (sL <= 31)),
        same * (sL <= tL),
        same * (sL > tL),
        same * ((sL >= tL).astype(np.float32) - (sL >= 32)),
        same * (sL >= tL),
        same * (sL < tL),
    ]
    c["c_L"] = np.concatenate([np.asarray(m, dtype=np.float32) for m in Ls], axis=1)
    c["c_sel"] = np.stack([(np.arange(128) // 64 == 0), (np.arange(128) // 64 == 1)], axis=1).astype(np.float32)
    return c


W_NAMES = ("norm1_g", "w_in", "lb_raw", "a_norm_g", "b_qn_g", "b_kn_g", "b_sink", "c_qn_g", "c_kn_g",
           "w_out", "norm2_g", "w_gate", "w_up", "w_down")

_CACHE = {}


def kernel(**inputs):
    xp = np.ascontiguousarray(np.asarray(inputs["x_prompt"], dtype=np.float32))
    xs = np.ascontiguousarray(np.asarray(inputs["x_sample"], dtype=np.float32))
    n = 8
    TS, TP = xs.shape[1], xp.shape[1]
    if "prog" not in _CACHE:
        P = Prog([TS, TP], 4)
        P.build()
        _CACHE["prog"] = P
    P = _CACHE["prog"]
    consts = make_consts(max(TS, TP))
    wts = {k: np.ascontiguousarray(np.asarray(inputs[k], dtype=np.float32)) for k in W_NAMES}
    in_maps = []
    for c in range(n):
        m = {"x0": xs[c], "x1": xp[c % xp.shape[0]]}
        m.update(wts)
        m.update(consts)
        in_maps.append(m)
    res = run_bass_kernel_spmd(P.nc, in_maps, core_ids=list(range(n)))
    y_s = np.stack([res.results[c]["y0"] for c in range(n)], axis=0)
    y_p = np.stack([res.results[c]["y1"] for c in range(xp.shape[0])], axis=0)
    return (y_p.astype(np.float32), y_s.astype(np.float32))
```

```python
import numpy as np
import concourse.bass as bass
import concourse.mybir as mybir
from concourse.bass_utils import run_bass_kernel_spmd
from contextlib import ExitStack

F32 = mybir.dt.float32
BF16 = mybir.dt.bfloat16
AF = mybir.ActivationFunctionType
ALU = mybir.AluOpType
AX = mybir.AxisListType

import os
KSTOP = int(os.environ.get("KSTOP", "0"))
D = 1024
DFF = 2816
NFF = DFF // 128
INW = 3072
EPS = 1e-6
FLOOR = 1e-30
OTW = 4352
O_DIR = 1024
O_V = 2048
O_GATE = 2304
O_BQ = 2560
O_BK = 2944
O_BV = 3072
O_C = 3200


class Buf:
    __slots__ = ("name", "t", "w", "r", "dsem", "dcnt", "psum")

    def __init__(self, name, t, psum=False):
        self.psum = psum
        self.name = name
        self.t = t
        self.w = None
        self.r = []
        self.dsem = None
        self.dcnt = 0

    def __getitem__(self, idx):
        return self.t[idx]


class Sched:
    ENG = ("pe", "act", "dve", "pool", "sp")

    def __init__(self, nc, es):
        self.nc = nc
        self.es = es
        self.sems = {}
        self.cnt = {}
        for e in self.ENG:
            self.sems[e] = es.enter_context(nc.semaphore("prog_" + e))
            self.cnt[e] = 0
        self.ndsem = 0
        self.free_dsem = {}
        self.dsem_eng = {}
        for eng, cnt in (("sp", 48), ("pool", 24)):
            fl = self.free_dsem.setdefault(eng, [])
            for _ in range(cnt):
                key = "d%s%d" % (eng, self.ndsem)
                self.ndsem += 1
                self.sems[key] = es.enter_context(nc.semaphore("dma_%s" % key))
                self.dsem_eng[key] = eng
                fl.append((key, 0))
        allsems = list(self.sems.values())
        with nc.Block() as block:
            @block.sync
            def _(e):
                for sm in allsems:
                    e.sem_clear(sm)
        self.epoch = 0
        self.known = {e: {} for e in self.ENG}
        self.q = {e: [] for e in self.ENG}
        self.dma_tokens = {}
        self.dma_all = {}
        self.ninst = 0

    def _dsem(self, buf, eng):
        if buf.dsem is None:
            fl = self.free_dsem.setdefault(eng, [])
            assert fl, "out of pre-allocated DMA semaphores for " + eng
            buf.dsem, buf.dcnt = fl.pop()
        assert self.dsem_eng[buf.dsem] == eng, "buffer DMA'd from two queues"
        return buf.dsem

    def release(self, bufs):
        for b in bufs:
            if b.dsem is not None:
                self.free_dsem[self.dsem_eng[b.dsem]].append((b.dsem, b.dcnt))
                b.dsem = None

    def _waits(self, eng, deps):
        kn = self.known[eng]
        need = {}
        for d in deps:
            if d is None:
                continue
            k, v, ep = d
            if k in self.cnt and ep != self.epoch:
                continue
            if kn.get(k, -1) < v and need.get(k, -1) < v:
                need[k] = v
        out = []
        for k, v in need.items():
            kn[k] = v
            out.append((k, v))
            if k in self.cnt:
                self.q[k][v]["signal"] = True
        return out

    def op(self, eng, fn, reads=(), writes=()):
        deps = []
        own_raw = set()
        for b in reads:
            deps.append(b.w)
            if b.w is not None and b.w[0] == eng:
                own_raw.add(b.w)
            if b.psum:
                deps.extend(b.r)
        for b in writes:
            deps.append(b.w)
            if b.w is not None and b.w[0] == eng and eng != "pe":
                own_raw.add(b.w)
            deps.extend(b.r)
        deps2 = [d for d in deps if d is not None and (d[0] != eng or d in own_raw)]
        w = self._waits(eng, deps2)
        idx = len(self.q[eng])
        tok = (eng, idx, self.epoch)
        self.q[eng].append({"kind": "op", "fn": fn, "waits": w, "signal": False})
        self.ninst += 1
        for b in reads:
            b.r.append(tok)
        for b in writes:
            b.w = tok
            b.r = []
        return tok

    def dma(self, eng, out, in_, sb, load, **kw):
        deps = [sb.w]
        if load:
            deps.extend(sb.r)
        w = self._waits(eng, [d for d in deps if d is not None])
        key = self._dsem(sb, eng)
        sb.dcnt += 16
        tok = (key, sb.dcnt, self.epoch)
        self.q[eng].append({"kind": "dma", "out": out, "in_": in_, "key": key, "kw": kw, "waits": w, "signal": False})
        self.dma_tokens[key] = sb.dcnt
        self.dma_all[key] = sb.dcnt
        self.ninst += 1
        if load:
            sb.w = tok
            sb.r = []
        else:
            sb.r.append(tok)
        return tok

    def flush(self):
        nc = self.nc
        fin = self._waits("sp", [(k, v, self.epoch) for k, v in self.dma_tokens.items()])
        self.dma_tokens = {}
        q = self.q
        sems = self.sems
        for e in self.ENG:
            c = self.cnt[e]
            for rec in q[e]:
                if rec["kind"] == "op" and rec["signal"]:
                    c += 1
                rec["val"] = c
            self.cnt[e] = c
        self.nsignal = getattr(self, "nsignal", 0) + sum(1 for e in self.ENG for r in q[e] if r["signal"])

        def emit_waits(e, waits):
            for k, v in waits:
                if k in self.cnt:
                    e.wait_ge(sems[k], q[k][v]["val"])
                else:
                    e.wait_ge(sems[k], v)

        def replay(ename, e):
            sem = sems[ename]
            for rec in q[ename]:
                emit_waits(e, rec["waits"])
                if rec["kind"] == "op":
                    ins = rec["fn"](e)
                    if rec["signal"]:
                        ins.then_inc(sem, 1)
                else:
                    e.dma_start(out=rec["out"], in_=rec["in_"], **rec["kw"]).then_inc(sems[rec["key"]], 16)
            if ename == "sp":
                emit_waits(e, fin)

        with nc.Block() as block:
            @block.tensor
            def _(e):
                replay("pe", e)

            @block.scalar
            def _(e):
                replay("act", e)

            @block.vector
            def _(e):
                replay("dve", e)

            @block.gpsimd
            def _(e):
                replay("pool", e)

            @block.sync
            def _(e):
                replay("sp", e)
        self.q = {e: [] for e in self.ENG}
        self.epoch += 1
        for e in self.ENG:
            self.known[e] = dict(self.dma_all)


class Phase:
    def __init__(self, P, name):
        self.P = P
        self.name = name
        self.es = ExitStack()
        self.bufs = []
        self.n = 0

    def sb(self, shape, dt, name=None):
        self.n += 1
        nm = "%s_%s%d" % (self.name, name or "s", self.n)
        b = Buf(nm, self.es.enter_context(self.P.nc.sbuf_tensor(nm, shape, dt)))
        self.bufs.append(b)
        return b

    def ps(self, shape, dt, name=None):
        self.n += 1
        nm = "%s_%s%d" % (self.name, name or "p", self.n)
        per = 512 if dt == F32 else 1024
        n = 1
        for d in shape[1:]:
            n *= d
        padded = -(-n // per) * per
        flat = self.es.enter_context(self.P.nc.psum_tensor(nm, [128, padded], dt))
        v = flat[:, 0:n]
        if len(shape) == 3:
            v = v.rearrange("p (a b) -> p a b", a=shape[1])
        elif len(shape) == 4:
            v = v.rearrange("p (a b c) -> p a b c", a=shape[1], b=shape[2])
        b = Buf(nm, v, psum=True)
        self.bufs.append(b)
        return b

    def close(self):
        self.P.S.flush()
        self.P.S.release(self.bufs)
        self.es.close()


def bc(ap, shape):
    return ap.broadcast_to(shape)


class Prog:
    def __init__(self, seqs, depth, nlayers_total=4, phases=("A", "H", "B", "C", "O", "F"), debug=False):
        self.debug = debug
        self.seqs = seqs
        self.depth = depth
        self.LT = nlayers_total
        self.phases = phases
        self.nc = bass.Bass("TRN2", target_bir_lowering=False)
        nc = self.nc
        self.din = {}

        def inp(name, shape):
            self.din[name] = nc.dram_tensor(name, shape, F32, kind="ExternalInput").ap()
            return self.din[name]

        LT = self.LT
        self.x_in = [inp("x%d" % i, [T, D]) for i, T in enumerate(seqs)]
        self.y_out = [nc.dram_tensor("y%d" % i, [T, D], F32, kind="ExternalOutput").ap()
                      for i, T in enumerate(seqs)]
        inp("norm1_g", [LT, D]); inp("w_in", [LT, D, INW]); inp("lb_raw", [LT, 2, 256])
        inp("a_norm_g", [LT, 64]); inp("b_qn_g", [LT, 64]); inp("b_kn_g", [LT, 64])
        inp("b_sink", [LT, 6]); inp("c_qn_g", [LT, 64]); inp("c_kn_g", [LT, 64])
        inp("w_out", [LT, 768, D]); inp("norm2_g", [LT, D])
        inp("w_gate", [LT, D, DFF]); inp("w_up", [LT, D, DFF]); inp("w_down", [LT, DFF, D])
        self.Tmax = max(seqs)
        inp("c_ident", [128, 128]); inp("c_rope", [self.Tmax, 128]); inp("c_mab", [128, 256])
        inp("c_mh", [128, 256]); inp("c_L", [128, 6 * 128]); inp("c_sel", [128, 2])

        def scr(name, shape, dt):
            return nc.dram_tensor(name, shape, dt, kind="ExternalOutput" if self.debug else "Internal").ap()

        self.PA = [scr("PA%d" % i, [T, OTW], BF16) for i, T in enumerate(seqs)]
        self.Z = [scr("Z%d" % i, [T // 128, 128, 8], F32) for i, T in enumerate(seqs)]
        self.OFB = [scr("OFB%d" % i, [2, T, 256], F32) for i, T in enumerate(seqs)]
        self.YB = [scr("YB%d" % i, [T, 384], BF16) for i, T in enumerate(seqs)]
        self.NZ = [scr("NZ%d" % i, [3, T, 130], F32) for i, T in enumerate(seqs)]
        self.H = [scr("H%d" % i, [T, D], F32) for i, T in enumerate(seqs)]
        self.X = [scr("X%d" % i, [T, D], F32) for i, T in enumerate(seqs)]

    def build(self):
        nc = self.nc
        with ExitStack() as es:
            self.S = Sched(nc, es)
            self.glob = Phase(self, "g")
            self.setup()
            for l in range(self.depth):
                xin = self.x_in if l == 0 else self.X
                xout = self.y_out if l == self.depth - 1 else self.X
                if "A" in self.phases:
                    self.phase_A(l, xin)
                if "H" in self.phases:
                    self.phase_H(l)
                if "B" in self.phases:
                    self.phase_B(l)
                if "C" in self.phases:
                    self.phase_C(l)
                if "O" in self.phases:
                    self.phase_O(l, xin)
                if "F" in self.phases:
                    self.phase_F(l, xout)
            self.S.flush()
            self.glob.es.close()
        return nc

    def setup(self):
        S, G, din = self.S, self.glob, self.din
        LT = self.LT
        self.idb = G.sb([128, 128], BF16, "idb")
        self.mab = G.sb([128, 256], BF16, "mab")
        self.mh = G.sb([128, 256], BF16, "mh")
        self.Lm = G.sb([128, 768], F32, "Lm")
        self.sel = G.sb([128, 2], F32, "sel")
        self.lb = G.sb([128, LT, 512], F32, "lb")
        self.omlb = G.sb([128, LT, 512], F32, "omlb")
        self.epsb = G.sb([128, 1], F32, "epsb")
        self.oneb = G.sb([128, 1], F32, "oneb")
        ph = Phase(self, "su")
        stg = ph.sb([128, 768], F32)
        S.op("dve", lambda e: e.memset(self.epsb[:], EPS), [], [self.epsb])
        S.op("dve", lambda e: e.memset(self.oneb[:], 1.0), [], [self.oneb])
        S.dma("sp", stg[:, 0:128], din["c_ident"], stg, True)
        S.dma("sp", stg[:, 128:384], din["c_mab"], stg, True)
        S.dma("sp", stg[:, 384:640], din["c_mh"], stg, True)
        S.dma("sp", self.Lm[:], din["c_L"], self.Lm, True)
        S.dma("sp", self.sel[:], din["c_sel"], self.sel, True)
        S.op("dve", lambda e: e.tensor_copy(out=self.idb[:], in_=stg[:, 0:128]), [stg], [self.idb])
        S.op("dve", lambda e: e.tensor_copy(out=self.mab[:], in_=stg[:, 128:384]), [stg], [self.mab])
        S.op("dve", lambda e: e.tensor_copy(out=self.mh[:], in_=stg[:, 384:640]), [stg], [self.mh])
        raw = ph.sb([128, LT, 512], F32)
        ex = ph.sb([128, LT, 512], F32)
        ssum = ph.sb([128, 512], F32)
        src = din["lb_raw"].rearrange("l d c -> (l d c)").unsqueeze(0).broadcast_to([128, LT * 512])
        S.dma("sp", raw[:].rearrange("p l c -> p (l c)"), src, raw, True)
        S.op("act", lambda e: e.activation(out=ex[:], in_=raw[:], func=AF.Exp), [raw], [ex])
        S.op("dve", lambda e: e.tensor_tensor(out=ssum[:], in0=ex[:, 0, :], in1=ex[:, 1, :], op=ALU.add), [ex], [ssum])
        for l in range(2, LT):
            S.op("dve", lambda e, l=l: e.tensor_tensor(out=ssum[:], in0=ssum[:], in1=ex[:, l, :], op=ALU.add), [ex, ssum], [ssum])
        S.op("dve", lambda e: e.reciprocal(out=ssum[:], in_=ssum[:]), [ssum], [ssum])
        S.op("dve", lambda e: e.memset(self.lb[:, 0, :], 0.0), [], [self.lb])
        for l in range(1, LT):
            S.op("dve", lambda e, l=l: e.tensor_tensor(out=ex[:, l, :], in0=ex[:, l, :], in1=ssum[:], op=ALU.mult), [ex, ssum], [ex])
            S.op("dve", lambda e, l=l: e.tensor_tensor(out=self.lb[:, l, :], in0=self.lb[:, l - 1, :], in1=ex[:, l, :], op=ALU.add), [ex, self.lb], [self.lb])
        S.op("dve", lambda e: e.tensor_scalar(out=self.omlb[:], in0=self.lb[:], scalar1=-1.0, scalar2=1.0, op0=ALU.mult, op1=ALU.add), [self.lb], [self.omlb])
        ph.close()

    def rstd_from_ss(self, ss, rstd, n):
        S = self.S
        S.op("act", lambda e: e.activation(out=rstd[:], in_=ss[:], func=AF.Sqrt, scale=1.0 / n, bias=self.epsb[:]), [ss, self.epsb], [rstd])
        S.op("dve", lambda e: e.reciprocal(out=rstd[:], in_=rstd[:]), [rstd], [rstd])

    def rstd_lnexp(self, ss, rstd, n):
        S = self.S
        S.op("act", lambda e: e.activation(out=rstd[:], in_=ss[:], func=AF.Ln, scale=1.0 / n, bias=self.epsb[:]), [ss, self.epsb], [rstd])
        S.op("act", lambda e: e.activation(out=rstd[:], in_=rstd[:], func=AF.Exp, scale=-0.5), [rstd], [rstd])

    def load_weight(self, ph, dst, src_rows, gvec, stage, segs, width):
        S = self.S
        KC = dst.t.shape[1]
        for kc in range(KC):
            st = stage[kc % len(stage)]
            S.dma("sp", st[:, 0:width], src_rows(kc), st, True)
            for i, (dc, sc, n) in enumerate(segs):
                eng = "dve" if (i + kc) % 2 == 0 else "act"
                if gvec is not None and eng == "act":
                    S.op(eng, lambda e, kc=kc, dc=dc, sc=sc, n=n, st=st: e.activation(
                        out=dst[:, kc, dc:dc + n], in_=st[:, sc:sc + n], func=AF.Copy, scale=gvec[:, kc:kc + 1]), [st, gvec], [dst])
                elif gvec is not None:
                    S.op(eng, lambda e, kc=kc, dc=dc, sc=sc, n=n, st=st: e.tensor_scalar(
                        out=dst[:, kc, dc:dc + n], in0=st[:, sc:sc + n], scalar1=gvec[:, kc:kc + 1], scalar2=None, op0=ALU.mult),
                        [st, gvec], [dst])
                elif eng == "act":
                    S.op(eng, lambda e, kc=kc, dc=dc, sc=sc, n=n, st=st: e.activation(
                        out=dst[:, kc, dc:dc + n], in_=st[:, sc:sc + n], func=AF.Copy), [st], [dst])
                else:
                    S.op(eng, lambda e, kc=kc, dc=dc, sc=sc, n=n, st=st: e.tensor_copy(
                        out=dst[:, kc, dc:dc + n], in_=st[:, sc:sc + n]), [st], [dst])

    def load_weight_simple(self, dst, src, gvec, stage, width, piece=1024):
        S = self.S
        KC = dst.t.shape[1]
        n = 0
        for kc in range(KC):
            for c0 in range(0, width, piece):
                w = min(piece, width - c0)
                ent = stage[n % len(stage)]
                st, sap = ent if isinstance(ent, tuple) else (ent, ent[:, :])
                S.dma("sp", sap[:, 0:w], src[kc * 128:(kc + 1) * 128, c0:c0 + w], st, True)
                eng = "dve" if n % 2 == 0 else "act"
                if eng == "act":
                    if gvec is not None:
                        S.op(eng, lambda e, kc=kc, c0=c0, w=w, sap=sap: e.activation(
                            out=dst[:, kc, c0:c0 + w], in_=sap[:, 0:w], func=AF.Copy, scale=gvec[:, kc:kc + 1]), [st, gvec], [dst])
                    else:
                        S.op(eng, lambda e, kc=kc, c0=c0, w=w, sap=sap: e.activation(out=dst[:, kc, c0:c0 + w], in_=sap[:, 0:w], func=AF.Copy), [st], [dst])
                elif gvec is not None:
                    S.op(eng, lambda e, kc=kc, c0=c0, w=w, sap=sap: e.tensor_scalar(
                        out=dst[:, kc, c0:c0 + w], in0=sap[:, 0:w], scalar1=gvec[:, kc:kc + 1], scalar2=None, op0=ALU.mult), [st, gvec], [dst])
                else:
                    S.op(eng, lambda e, kc=kc, c0=c0, w=w, sap=sap: e.tensor_copy(out=dst[:, kc, c0:c0 + w], in_=sap[:, 0:w]), [st], [dst])
                n += 1

    def phase_A(self, l, xin):
        S, din = self.S, self.din
        ph = Phase(self, "A%d" % l)
        wi = ph.sb([128, 8, INW], BF16, "wi")
        stage = [ph.sb([128, INW], F32, "stg") for _ in range(2)]
        g1 = ph.sb([128, 8], F32, "g1")
        S.dma("sp", g1[:], din["norm1_g"][l].rearrange("(c p) -> p c", p=128), g1, True, allow_slow_non_contiguous=True)
        segs = [(0, 256, 512), (512, 0, 256), (768, 768, 256), (1024, 1024, 256)]
        dc = 1280
        for h in (0, 3, 1, 4, 2, 5):
            segs.append((dc, 1280 + 64 * h, 64)); dc += 64
        segs.append((dc, 1664, 128)); dc += 128
        segs.append((dc, 1920, 384)); dc += 384
        segs.append((dc, 2304, 384)); dc += 384
        segs.append((dc, 1792, 128)); dc += 128
        segs.append((dc, 2688, 384)); dc += 384
        assert dc == INW
        self.load_weight(ph, wi, lambda kc: din["w_in"][l, kc * 128:(kc + 1) * 128, :], g1, stage, segs, INW)
        Gqk = ph.sb([128, 20, 64], F32, "Gqk")
        gs = ph.sb([128, 4, 64], F32, "gs")
        for j, nm in enumerate(("b_qn_g", "b_kn_g", "c_qn_g", "c_kn_g")):
            S.dma("sp", gs[:, j, :], din[nm][l:l + 1, :].broadcast_to([128, 64]), gs, True)
        for (h0, n, j, sc) in ((0, 6, 0, 0.125), (6, 2, 1, 1.0), (8, 6, 2, 0.125), (14, 6, 3, 1.0)):
            S.op("dve", lambda e, h0=h0, n=n, j=j, sc=sc: e.tensor_scalar(
                out=Gqk[:, h0:h0 + n, :], in0=gs[:, j, :].unsqueeze(1).broadcast_to([128, n, 64]),
                scalar1=sc, scalar2=None, op0=ALU.mult), [gs], [Gqk])
        lbv = self.lb[:, l, :]
        omv = self.omlb[:, l, :]

        xt = [ph.sb([128, D], F32, "xt") for _ in range(2)]
        cs = [ph.sb([128, 128], F32, "cs") for _ in range(3)]
        junk = ph.sb([128, D], BF16, "junk")
        ss = ph.sb([128, 1], F32); rstd = ph.sb([128, 1], F32)
        xb = [ph.sb([128, D], BF16, "xb") for _ in range(2)]
        xT = [ph.sb([128, 8, 128], BF16, "xT") for _ in range(2)]
        pT = ph.ps([128, 8, 128], BF16, "pT")
        pj = [ph.ps([128, 512], F32, "pj") for _ in range(3)]
        pc = ph.ps([128, 6, 256], F32, "pc")
        pz = ph.ps([128, 8], F32, "pz")
        sg = ph.sb([128, 512], F32, "sg"); t1 = ph.sb([128, 512], F32, "t1")
        ff = ph.sb([128, 512], F32, "ff"); kk2b = [ph.sb([128, 512], F32, "kk") for _ in range(2)]
        gl = ph.sb([128, 512], F32, "gl")
        ghi = ph.sb([128, 512], BF16, "ghi"); glo = ph.sb([128, 512], BF16, "glo")
        Lmb = ph.sb([128, 768], BF16, "Lmb"); selb = ph.sb([128, 2], BF16, "selb")
        S.op("dve", lambda e: e.tensor_copy(out=Lmb[:], in_=self.Lm[:]), [self.Lm], [Lmb])
        S.op("dve", lambda e: e.tensor_copy(out=selb[:], in_=self.sel[:]), [self.sel], [selb])
        qi2b = [ph.sb([128, 256], F32, "qi") for _ in range(2)]
        sgl = ph.sb([128, 256], F32, "sgl")
        pq2b = [ph.sb([128, 1536], F32, "pq") for _ in range(2)]
        sq = ph.sb([128, 20, 64], F32, "sq")
        qn = ph.sb([128, 20, 64], F32, "qn")
        ra = ph.sb([128, 20, 64], F32, "ra"); rb = ph.sb([128, 20, 64], F32, "rb")
        ssq = ph.sb([128, 20], F32); rs2b = [ph.sb([128, 20], F32) for _ in range(2)]
        Epos = ph.sb([128, 6, 256], F32, "Epos"); Eneg = ph.sb([128, 2, 256], F32, "Eneg")
        zt = [ph.sb([128, 8], F32, "zt") for _ in range(2)]
        ot = [ph.sb([128, OTW], BF16, "ot") for _ in range(2)]
        Lm = self.Lm

        tiles = [(si, i) for si, T in enumerate(self.seqs) for i in range(T // 128)]

        def norm(n):
            si, i = tiles[n]
            x = xt[n % 2]
            xbn = xb[n % 2]
            S.dma("sp", x[:], xin[si][i * 128:(i + 1) * 128, :], x, True)
            S.dma("sp", cs[n % 3][:], din["c_rope"][i * 128:(i + 1) * 128, :], cs[n % 3], True)
            S.op("act", lambda e: e.activation(out=junk[:], in_=x[:], func=AF.Square, accum_out=ss[:]), [x], [junk, ss])
            self.rstd_lnexp(ss, rstd, D)
            S.op("dve", lambda e: e.tensor_scalar(out=xbn[:], in0=x[:], scalar1=rstd[:], scalar2=None, op0=ALU.mult), [x, rstd], [xbn])

        def trans(n):
            xbn = xb[n % 2]
            for c in range(8):
                S.op("pe", lambda e, c=c: e.transpose(out=pT[:, c, :], in_=xbn[:, c * 128:(c + 1) * 128], identity=self.idb[:]), [xbn, self.idb], [pT])
            xTn = xT[n % 2]
            S.op("act", lambda e: e.activation(out=xTn[:], in_=pT[:], func=AF.Copy), [pT], [xTn])

        def main(n):
            si, i = tiles[n]
            xTn = xT[n % 2]
            o = ot[n % 2]
            c_ = cs[n % 3]
            z = zt[n % 2]
            kk, qi, pq, rs = kk2b[n % 2], qi2b[n % 2], pq2b[n % 2], rs2b[n % 2]
            prev = pending.pop(n - 1, [[] for _ in range(6)])
            for ch in range(6):
                p = pj[ch % 3]
                for kc in range(8):
                    S.op("pe", lambda e, p=p, kc=kc, ch=ch: e.matmul(p[:], lhsT=xTn[:, kc, :], rhs=wi[:, kc, ch * 512:(ch + 1) * 512], start=(kc == 0), stop=(kc == 7)), [xTn, wi], [p])
                if ch == 0:
                    S.op("act", lambda e, p=p: e.activation(out=sg[:], in_=p[:], func=AF.Exp, scale=-1.0), [p], [sg])
                elif ch == 1:
                    S.op("act", lambda e, p=p: e.activation(out=qi[:], in_=p[:, 0:256], func=AF.Copy), [p], [qi])
                    S.op("act", lambda e, p=p: e.activation(out=o[:, O_V:O_V + 256], in_=p[:, 256:512], func=AF.Copy), [p], [o])
                    S.op("act", lambda e: e.activation(out=sg[:], in_=sg[:], func=AF.Ln, bias=self.oneb[:]), [sg, self.oneb], [sg])
                    S.op("act", lambda e: e.activation(out=sg[:], in_=sg[:], func=AF.Exp, scale=-1.0), [sg], [sg])
                    S.op("dve", lambda e: e.tensor_tensor(out=t1[:], in0=sg[:], in1=omv, op=ALU.mult), [sg, self.omlb], [t1])
                    S.op("dve", lambda e: e.scalar_tensor_tensor(out=ff[:], in0=t1[:], scalar=FLOOR, in1=lbv, op0=ALU.max, op1=ALU.add), [t1, self.lb], [ff])
                    S.op("dve", lambda e: e.tensor_tensor(out=kk[:], in0=omv, in1=t1[:], op=ALU.subtract), [t1, self.omlb], [kk])
                elif ch in (2, 3, 4):
                    S.op("act", lambda e, p=p, ch=ch: e.activation(out=pq[:, (ch - 2) * 512:(ch - 1) * 512], in_=p[:], func=AF.Copy), [p], [pq])
                    if ch == 3:
                        S.op("act", lambda e: e.activation(out=gl[:], in_=ff[:], func=AF.Ln), [ff], [gl])
                        S.op("act", lambda e: e.activation(out=ghi[:], in_=gl[:], func=AF.Copy), [gl], [ghi])
                        S.op("dve", lambda e: e.tensor_tensor(out=glo[:], in0=gl[:], in1=ghi[:], op=ALU.subtract), [gl, ghi], [glo])
                    if ch == 4:
                        S.op("act", lambda e: e.activation(out=sq[:], in_=pq[:, 256:1536].rearrange("p (h c) -> p h c", h=20), func=AF.Square), [pq], [sq])
                        S.op("dve", lambda e: e.tensor_reduce(out=ssq[:], in_=sq[:], axis=AX.X, op=ALU.add), [sq], [ssq])
                else:
                    S.op("act", lambda e, p=p: e.activation(out=o[:, O_BV:O_BV + 128], in_=p[:, 0:128], func=AF.Copy), [p], [o])
                    S.op("act", lambda e, p=p: e.activation(
                        out=o[:, O_C:O_C + 1152].rearrange("p (g c) -> p g c", g=3)[:, :, 256:384],
                        in_=p[:, 128:512].rearrange("p (g c) -> p g c", g=3), func=AF.Copy), [p], [o])
                    self.rstd_lnexp(ssq, rs, 64)
                for st in prev[ch]:
                    st()
            for j in range(6):
                d = j // 3
                for part, gsrc in enumerate((ghi, glo)):
                    S.op("pe", lambda e, j=j, d=d, part=part, gsrc=gsrc: e.matmul(pc[:, j, :], lhsT=Lmb[:, j * 128:(j + 1) * 128], rhs=gsrc[:, d * 256:(d + 1) * 256], start=(part == 0), stop=(part == 1)), [Lmb, gsrc], [pc])
            for d in range(2):
                for hp in range(2):
                    idx = d * 2 + hp
                    for part, gsrc in enumerate((ghi, glo)):
                        S.op("pe", lambda e, d=d, hp=hp, idx=idx, part=part, gsrc=gsrc: e.matmul(pz[:, idx * 2:idx * 2 + 2], lhsT=gsrc[:, d * 256 + hp * 128:d * 256 + hp * 128 + 128], rhs=selb[:], start=(part == 0), stop=(part == 1)), [gsrc, selb], [pz])
            if n + 1 < nt:
                trans(n + 1)
            if n + 2 < nt:
                norm(n + 2)
            od = o[:, 0:2048].rearrange("p (d c) -> p d c", d=2)
            q2 = qi[:, :].unsqueeze(1).broadcast_to([128, 2, 256])
            kk2 = kk[:].rearrange("p (d c) -> p d c", d=2)
            pq2 = pq[:, 256:1536].rearrange("p (h c) -> p h c", h=20)
            cc = c_[:, 0:64].unsqueeze(1).broadcast_to([128, 20, 64])
            msin = c_[:, 64:96].unsqueeze(1).broadcast_to([128, 20, 32])
            psin = c_[:, 96:128].unsqueeze(1).broadcast_to([128, 20, 32])
            oc = o[:, O_C:O_C + 1152].rearrange("p (g c) -> p g c", g=3)
            P0 = [
                lambda: S.op("act", lambda e: e.activation(out=Epos[:], in_=pc[:], func=AF.Exp), [pc], [Epos]),
                lambda: S.op("act", lambda e: e.activation(out=Eneg[:], in_=pc[:, 0:6:3, :], func=AF.Exp, scale=-1.0), [pc], [Eneg]),
                lambda: S.op("act", lambda e: e.activation(out=z[:], in_=pz[:], func=AF.Exp), [pz], [z]),
                lambda: S.dma("pool", self.Z[si][i], z[:], z, False),
                lambda: S.op("act", lambda e: e.activation(out=sgl[:], in_=pq[:, 0:256], func=AF.Exp, scale=-1.0), [pq], [sgl]),
                lambda: S.op("act", lambda e: e.activation(out=sgl[:], in_=sgl[:], func=AF.Ln, bias=self.oneb[:]), [sgl, self.oneb], [sgl]),
                lambda: S.op("act", lambda e: e.activation(out=sgl[:], in_=sgl[:], func=AF.Exp, scale=-1.0), [sgl], [sgl]),
            ]
            P1 = [
                lambda: S.op("dve", lambda e: e.tensor_tensor(out=od[:, :, 0:256], in0=q2, in1=Epos[:, 0:6:3, :], op=ALU.mult), [qi, Epos], [o]),
                lambda: S.op("dve", lambda e: e.tensor_tensor(out=od[:, :, 256:512], in0=kk2, in1=Eneg[:], op=ALU.mult), [kk, Eneg], [o]),
                lambda: S.op("dve", lambda e: e.tensor_tensor(out=od[:, :, 512:768], in0=q2, in1=Epos[:, 1:6:3, :], op=ALU.mult), [qi, Epos], [o]),
                lambda: S.op("dve", lambda e: e.tensor_tensor(out=od[:, :, 768:1024], in0=kk2, in1=Epos[:, 2:6:3, :], op=ALU.mult), [kk, Epos], [o]),
                lambda: S.op("dve", lambda e: e.tensor_tensor(out=o[:, O_GATE:O_GATE + 256], in0=pq[:, 0:256], in1=sgl[:], op=ALU.mult), [pq, sgl], [o]),
            ]
            P2 = [
                lambda: S.op("dve", lambda e: e.tensor_tensor(out=qn[:], in0=pq2, in1=rs[:, :].unsqueeze(2).broadcast_to([128, 20, 64]), op=ALU.mult), [pq, rs], [qn]),
                lambda: S.op("dve", lambda e: e.tensor_tensor(out=qn[:], in0=qn[:], in1=Gqk[:], op=ALU.mult), [qn, Gqk], [qn]),
            ]
            P3 = [
                lambda: S.op("dve", lambda e: e.tensor_tensor(out=ra[:], in0=qn[:], in1=cc, op=ALU.mult), [qn, c_], [ra]),
                lambda: S.op("dve", lambda e: e.tensor_tensor(out=rb[:, :, 0:32], in0=qn[:, :, 32:64], in1=msin, op=ALU.mult), [qn, c_], [rb]),
                lambda: S.op("dve", lambda e: e.tensor_tensor(out=rb[:, :, 32:64], in0=qn[:, :, 0:32], in1=psin, op=ALU.mult), [qn, c_], [rb]),
            ]
            P4 = [
                lambda: S.op("dve", lambda e: e.tensor_tensor(out=o[:, O_BQ:O_BQ + 512].rearrange("p (h c) -> p h c", h=8), in0=ra[:, 0:8, :], in1=rb[:, 0:8, :], op=ALU.add), [ra, rb], [o]),
                lambda: S.op("dve", lambda e: e.tensor_tensor(out=oc[:, :, 0:128], in0=ra[:, 8:14, :].rearrange("p (g h) c -> p g (h c)", g=3),
                                                              in1=rb[:, 8:14, :].rearrange("p (g h) c -> p g (h c)", g=3), op=ALU.add), [ra, rb], [o]),
                lambda: S.op("dve", lambda e: e.tensor_tensor(out=oc[:, :, 128:256], in0=ra[:, 14:20, :].rearrange("p (g h) c -> p g (h c)", g=3),
                                                              in1=rb[:, 14:20, :].rearrange("p (g h) c -> p g (h c)", g=3), op=ALU.add), [ra, rb], [o]),
                lambda: S.dma("pool", self.PA[si][i * 128:(i + 1) * 128, :], o[:], o, False),
            ]
            pending[n] = [P0, P1, P2, P3, P4, []]

        pending = {}
        nt = len(tiles)
        norm(0)
        trans(0)
        if nt > 1:
            norm(1)
        for n in range(nt):
            main(n)
        for grp in pending.pop(nt - 1):
            for st in grp:
                st()
        ph.close()

    def phase_H(self, l):
        S = self.S
        ph = Phase(self, "H%d" % l)
        NL = 3
        hin = [[ph.sb([128, 1280], BF16, "hin") for _ in range(NL)] for d in range(2)]
        vz = [[[ph.sb([128, 256], BF16, "vz") for j in range(2)] for _ in range(NL)] for d in range(2)]
        zt = [[ph.sb([128, 4], F32, "zt") for _ in range(NL)] for d in range(2)]
        hT = [[ph.sb([128, 6, 128], BF16, "hT") for _ in range(2)] for d in range(2)]
        Am = [[ph.sb([128, 4, 128], BF16, "Am") for _ in range(2)] for d in range(2)]
        St = [ph.sb([128, 2, 64], F32, "St") for d in range(2)]
        Sb = [[ph.sb([128, 2, 64], BF16, "Sb") for _ in range(2)] for d in range(2)]
        osb = [[ph.sb([128, 4, 64], F32, "osb") for _ in range(2)] for d in range(2)]
        pT = ph.ps([128, 6, 128], BF16, "pT")
        pA = [ph.ps([128, 2, 128], F32, "pA") for r in range(2)]
        pO = [ph.ps([128, 2, 2, 64], F32, "pO") for r in range(2)]
        pS = [ph.ps([128, 2, 64], F32, "pS") for d in range(2)]
        idb, mh = self.idb, self.mh
        for d in range(2):
            for sl in range(NL):
                for j in range(2):
                    S.op("pool", lambda e, b=vz[d][sl][j]: e.memset(b[:], 0.0), [], [vz[d][sl][j]])
        for si, T in enumerate(self.seqs):
            n = T // 128
            sbi = [0, 0]
            for d in range(2):
                S.op("dve", lambda e, d=d: e.memset(St[d][:], 0.0), [], [St[d]])
                S.op("dve", lambda e, d=d: e.memset(Sb[d][0][:], 0.0), [], [Sb[d][0]])

            def load(step):
                for d in range(2):
                    ti = step if d == 0 else n - 1 - step
                    h = hin[d][step % NL]
                    S.dma("sp", h[:, 0:1024], self.PA[si][ti * 128:(ti + 1) * 128, d * 1024:(d + 1) * 1024], h, True)
                    S.dma("sp", h[:, 1024:1280], self.PA[si][ti * 128:(ti + 1) * 128, O_V:O_V + 256], h, True)
                    for j in range(2):
                        v = vz[d][step % NL][j]
                        S.dma("sp", v[j * 64:(j + 1) * 64, :], self.PA[si][ti * 128 + j * 64:ti * 128 + (j + 1) * 64, O_V:O_V + 256], v, True)
                    z = zt[d][step % NL]
                    S.dma("sp", z[:], self.Z[si][ti][:, d * 4:(d + 1) * 4], z, True)

            def prep(step, d):
                h = hin[d][step % NL]
                hTd = hT[d][step % 2]
                Amd = Am[d][step % 2]
                for j in range(3):
                    for hp in range(2):
                        S.op("pe", lambda e, j=j, hp=hp, h=h: e.transpose(out=pT[:, j * 2 + hp, :], in_=h[:, j * 256 + hp * 128: j * 256 + hp * 128 + 128], identity=idb[:]), [h, idb], [pT])
                S.op("act", lambda e: e.activation(out=hTd[:], in_=pT[:], func=AF.Copy), [pT], [hTd])
                for hh in range(4):
                    hp, r = hh // 2, hh % 2
                    kb = r * 64
                    S.op("pe", lambda e, hp=hp, r=r, kb=kb: e.matmul(
                        pA[r][:, hp, :], lhsT=hTd[kb:kb + 64, 2 + hp, :], rhs=hTd[kb:kb + 64, 0 + hp, :], start=True, stop=True), [hTd], [pA[r]])
                for r in range(2):
                    S.op("dve", lambda e, r=r: e.tensor_tensor(
                        out=Amd[:, r:4:2, :], in0=pA[r][:], in1=mh[:, d * 128:(d + 1) * 128].unsqueeze(1).broadcast_to([128, 2, 128]), op=ALU.mult), [pA[r], mh], [Amd])

            def chain(step):
                for jj in range(2):
                    for d in range(2):
                        j = jj if d == 0 else 1 - jj
                        h = hin[d][step % NL]
                        hTd = hT[d][step % 2]
                        Amd = Am[d][step % 2]
                        sbcur = Sb[d][sbi[d] % 2]
                        vzj = vz[d][step % NL][j]
                        for hh in range(4):
                            hp, r = hh // 2, hh % 2
                            kb = r * 64
                            po = pO[r]
                            S.op("pe", lambda e, d=d, j=j, hh=hh, hp=hp, h=h, po=po, Amd=Amd: e.matmul(
                                po[j * 64:(j + 1) * 64, d, hp, :], lhsT=Amd[:, hh, j * 64:(j + 1) * 64],
                                rhs=h[:, 1024 + hh * 64:1024 + (hh + 1) * 64], start=True, stop=False), [Amd, h], [po])
                            S.op("pe", lambda e, d=d, j=j, hp=hp, kb=kb, sbcur=sbcur, po=po, hTd=hTd: e.matmul(
                                po[j * 64:(j + 1) * 64, d, hp, :], lhsT=hTd[kb:kb + 64, 4 + hp, j * 64:(j + 1) * 64],
                                rhs=sbcur[kb:kb + 64, hp, :], start=False, stop=True), [hTd, sbcur], [po])
                        for hh in range(4):
                            hp, r = hh // 2, hh % 2
                            kb = r * 64
                            S.op("pe", lambda e, d=d, hh=hh, hp=hp, kb=kb, h=h, vzj=vzj: e.matmul(
                                pS[d][kb:kb + 64, hp, :], lhsT=h[:, 768 + hh * 64:768 + (hh + 1) * 64],
                                rhs=vzj[:, hh * 64:(hh + 1) * 64], start=True, stop=True), [h, vzj], [pS[d]])
                    for d in range(2):
                        j = jj if d == 0 else 1 - jj
                        z = zt[d][step % NL]
                        sbi[d] += 1
                        sbn = Sb[d][sbi[d] % 2]
                        for hp in range(2):
                            S.op("dve", lambda e, d=d, j=j, hp=hp, z=z, sbn=sbn: e.scalar_tensor_tensor(
                                out=sbn[:, hp, :], in0=St[d][:, hp, :], scalar=z[:, hp * 2 + j:hp * 2 + j + 1], in1=pS[d][:, hp, :],
                                op0=ALU.mult, op1=ALU.add), [St[d], z, pS[d]], [sbn])
                        for hp in range(2):
                            S.op("dve", lambda e, d=d, j=j, hp=hp, z=z: e.scalar_tensor_tensor(
                                out=St[d][:, hp, :], in0=St[d][:, hp, :], scalar=z[:, hp * 2 + j:hp * 2 + j + 1], in1=pS[d][:, hp, :],
                                op0=ALU.mult, op1=ALU.add), [St[d], z, pS[d]], [St[d]])
                for d in range(2):
                    ti = step if d == 0 else n - 1 - step
                    ob = osb[d][step % 2]
                    for r in range(2):
                        S.op("act", lambda e, d=d, r=r, ob=ob: e.activation(out=ob[:, r:4:2, :], in_=pO[r][:, d, :, :], func=AF.Copy), [pO[r]], [ob])
                    S.dma("pool", self.OFB[si][d, ti * 128:(ti + 1) * 128, :], ob[:].rearrange("p h c -> p (h c)"), ob, False)

            load(0)
            if n > 1:
                load(1)
            prep(0, 0)
            prep(0, 1)
            for step in range(n):
                if step + 2 < n:
                    load(step + 2)
                if step + 1 < n:
                    prep(step + 1, 0)
                    prep(step + 1, 1)
                chain(step)
        ph.close()

    def phase_B(self, l):
        S, din = self.S, self.din
        ph = Phase(self, "B%d" % l)
        esk = ph.sb([128, 6], F32, "esk")
        S.dma("sp", esk[:], din["b_sink"][l:l + 1, :].broadcast_to([128, 6]), esk, True)
        S.op("act", lambda e: e.activation(out=esk[:], in_=esk[:], func=AF.Exp), [esk], [esk])
        NS = 5
        kt = [ph.sb([128, 128], BF16, "kt") for _ in range(2)]
        kT = [ph.sb([128, 128], BF16, "kT") for _ in range(NS)]
        v1 = [ph.sb([128, 2, 65], BF16, "v1") for _ in range(NS)]
        qt = [ph.sb([128, 384], BF16, "qt") for _ in range(3)]
        qT = [ph.sb([128, 3, 128], BF16, "qT") for _ in range(2)]
        Pt = [[ph.sb([128, 3, 384], BF16, "Pt") for _ in range(2)] for hk in range(2)]
        den = ph.sb([128, 6], F32, "den")
        yb = [ph.sb([128, 6, 64], BF16, "yb") for _ in range(2)]
        pS = [ph.ps([128, 3, 512], F32, "pS") for _ in range(2)]
        pT = ph.ps([128, 4, 128], BF16, "pT")
        pO = ph.ps([128, 6, 65], F32, "pO")
        idb, mab = self.idb, self.mab
        for b in v1:
            S.op("dve", lambda e, b=b: e.memset(b[:], 1.0), [], [b])
        for si, T in enumerate(self.seqs):
            n = T // 128

            def loadk(m):
                k = kt[m % 2]
                S.dma("sp", k[:], self.PA[si][m * 128:(m + 1) * 128, O_BK:O_BK + 128], k, True)
                S.dma("sp", v1[m % NS][:, :, 0:64], self.PA[si][m * 128:(m + 1) * 128, O_BV:O_BV + 128].rearrange("p (h c) -> p h c", h=2), v1[m % NS], True)
                S.op("pe", lambda e, k=k: e.transpose(out=pT[:, 3, :], in_=k[:], identity=idb[:]), [k, idb], [pT])
                S.op("act", lambda e, m=m: e.activation(out=kT[m % NS][:], in_=pT[:, 3, :], func=AF.Copy), [pT], [kT[m % NS]])

            def loadq(i):
                q = qt[i % 3]
                S.dma("sp", q[:], self.PA[si][i * 128:(i + 1) * 128, O_BQ:O_BQ + 384], q, True)

            def front(i):
                q = qt[i % 3]
                qTi = qT[i % 2]
                for p in range(3):
                    S.op("pe", lambda e, p=p, q=q: e.transpose(out=pT[:, p, :], in_=q[:, p * 128:(p + 1) * 128], identity=idb[:]), [q, idb], [pT])
                S.op("act", lambda e: e.activation(out=qTi[:], in_=pT[:, 0:3, :], func=AF.Copy), [pT], [qTi])
                ms = [m for m in (i - 1, i, i + 1) if 0 <= m < n]
                for hk in range(2):
                    ps_ = pS[hk]
                    P_ = Pt[hk][i % 2]
                    for mi, m in enumerate(ms):
                        S.op("pe", lambda e, hk=hk, mi=mi, m=m, ps_=ps_: e.matmul(
                            ps_[:, mi, 0:384], lhsT=kT[m % NS][hk * 64:(hk + 1) * 64, :],
                            rhs=qTi[hk * 64:(hk + 1) * 64, :, :].rearrange("p a b -> p (a b)"), start=True, stop=True), [kT[m % NS], qTi], [ps_])
                    nm = len(ms)
                    S.op("act", lambda e, ps_=ps_, P_=P_, nm=nm: e.activation(out=P_[:, 0:nm, :], in_=ps_[:, 0:nm, 0:384], func=AF.Exp), [ps_], [P_])
                    for mi, m in enumerate(ms):
                        if m == i:
                            continue
                        mk = mab[:, 0:128] if m < i else mab[:, 128:256]
                        S.op("dve", lambda e, mi=mi, mk=mk, P_=P_: e.tensor_tensor(
                            out=P_[:, mi, :].rearrange("p (h a) -> p h a", h=3), in0=P_[:, mi, :].rearrange("p (h a) -> p h a", h=3),
                            in1=mk.unsqueeze(1).broadcast_to([128, 3, 128]), op=ALU.mult), [P_, mab], [P_])

            def back(i):
                ms = [m for m in (i - 1, i, i + 1) if 0 <= m < n]
                for hk in range(2):
                    P_ = Pt[hk][i % 2]
                    for p in range(3):
                        for mi, m in enumerate(ms):
                            S.op("pe", lambda e, hk=hk, p=p, mi=mi, m=m, P_=P_: e.matmul(
                                pO[:, hk * 3 + p, :], lhsT=P_[:, mi, p * 128:(p + 1) * 128], rhs=v1[m % NS][:, hk, :],
                                start=(mi == 0), stop=(mi == len(ms) - 1)), [P_, v1[m % NS]], [pO])
                S.op("dve", lambda e: e.tensor_tensor(out=den[:], in0=pO[:, :, 64], in1=esk[:], op=ALU.add), [pO, esk], [den])
                S.op("dve", lambda e: e.reciprocal(out=den[:], in_=den[:]), [den], [den])
                y = yb[i % 2]
                S.op("dve", lambda e, y=y: e.tensor_tensor(out=y[:], in0=pO[:, :, 0:64], in1=den[:, :].unsqueeze(2).broadcast_to([128, 6, 64]), op=ALU.mult), [pO, den], [y])
                S.dma("pool", self.YB[si][i * 128:(i + 1) * 128, :], y[:].rearrange("p h c -> p (h c)"), y, False)

            loadk(0); loadq(0)
            if n > 1:
                loadk(1); loadq(1)
            front(0)
            for i in range(n):
                if i + 2 < n:
                    loadk(i + 2); loadq(i + 2)
                if i + 1 < n:
                    front(i + 1)
                back(i)
        ph.close()

    def phase_C(self, l):
        S = self.S
        ph = Phase(self, "C%d" % l)
        NS = 6
        kt = [ph.sb([128, 128], BF16, "kt") for _ in range(2)]
        kT = [ph.sb([128, 128], BF16, "kT") for _ in range(NS)]
        v1 = [ph.sb([128, 2, 65], BF16, "v1") for _ in range(NS)]
        v1e = [[ph.sb([128, 2, 65], BF16, "v1e") for _ in range(3)] for e_ in range(2)]
        qt = [ph.sb([128, 128], BF16, "qt") for _ in range(3)]
        qT = [ph.sb([128, 128], BF16, "qT") for _ in range(2)]
        Pt = [ph.sb([128, 2, 2, 128], BF16, "Pt") for _ in range(2)]
        nz = [ph.sb([128, 130], F32, "nz") for _ in range(2)]
        pT = [ph.ps([128, 2, 128], BF16, "pT") for _ in range(2)]
        pS = [[ph.ps([128, 2, 128], F32, "pS") for hh in range(2)] for _ in range(2)]
        pO = [ph.ps([128, 2, 65], F32, "pO") for _ in range(2)]
        idb, mab = self.idb, self.mab
        for b in v1:
            S.op("dve", lambda e, b=b: e.memset(b[:], 1.0), [], [b])
        for e_ in range(2):
            for b in v1e[e_]:
                S.op("dve", lambda e, b=b: e.memset(b[:], 1.0), [], [b])
                lo = 0 if e_ == 0 else 64
                S.op("dve", lambda e, b=b, lo=lo: e.memset(b[lo:lo + 64, :, :], 0.0), [], [b])
        for b in kt:
            S.op("dve", lambda e, b=b: e.memset(b[:], 0.0), [], [b])
        cnt = [0, 0, 0]
        jobs = []
        for si, T in enumerate(self.seqs):
            for g, dil in enumerate((1, 4, 16)):
                nq = (T // dil) // 128
                for r in range(dil):
                    grp = {"si": si, "g": g, "dil": dil, "nq": nq, "r": r, "kslot": {}, "vbuf": {},
                           "pav": self.PA[si].rearrange("(j d) c -> d j c", d=dil),
                           "nzv": self.NZ[si][g].rearrange("(j d) c -> d j c", d=dil), "c0": O_C + 384 * g}
                    for b in range(nq):
                        jobs.append((grp, b))

        def loadk(grp, m):
            nq, r, pav, c0 = grp["nq"], grp["r"], grp["pav"], grp["c0"]
            lo = 64 if m == 0 else 0
            hi = 64 if m == nq else 128
            j0 = m * 128 - 64 + lo
            j1 = m * 128 - 64 + hi
            ks = cnt[0] % NS
            k = kt[cnt[0] % 2]
            cnt[0] += 1
            grp["kslot"][m] = ks
            if m == 0:
                vb = v1e[0][cnt[1] % 3]; cnt[1] += 1
            elif m == nq:
                vb = v1e[1][cnt[2] % 3]; cnt[2] += 1
            else:
                vb = v1[ks]
            grp["vbuf"][m] = vb
            S.dma("sp", k[lo:hi, :], pav[r, j0:j1, c0 + 128:c0 + 256], k, True)
            S.dma("sp", vb[lo:hi, :, 0:64], pav[r, j0:j1, c0 + 256:c0 + 384].rearrange("p (h c) -> p h c", h=2), vb, True)
            pt_ = pT[cnt[0] % 2]
            S.op("pe", lambda e, k=k, pt_=pt_: e.transpose(out=pt_[:, 1, :], in_=k[:], identity=idb[:]), [k, idb], [pt_])
            S.op("act", lambda e, ks=ks, pt_=pt_: e.activation(out=kT[ks][:], in_=pt_[:, 1, :], func=AF.Copy), [pt_], [kT[ks]])

        def load(k):
            grp, b = jobs[k]
            if b == 0:
                loadk(grp, 0)
            loadk(grp, b + 1)
            q = qt[k % 3]
            S.dma("sp", q[:], grp["pav"][grp["r"], b * 128:(b + 1) * 128, grp["c0"]:grp["c0"] + 128], q, True)

        def front(k):
            grp, b = jobs[k]
            q = qt[k % 3]
            qTi = qT[k % 2]
            pt_ = pT[k % 2]
            P_ = Pt[k % 2]
            S.op("pe", lambda e: e.transpose(out=pt_[:, 0, :], in_=q[:], identity=idb[:]), [q, idb], [pt_])
            S.op("act", lambda e: e.activation(out=qTi[:], in_=pt_[:, 0, :], func=AF.Copy), [pt_], [qTi])
            for hh in range(2):
                ps_ = pS[k % 2][hh]
                for u in range(2):
                    ks = grp["kslot"][b + u]
                    S.op("pe", lambda e, hh=hh, u=u, ks=ks, ps_=ps_: e.matmul(
                        ps_[:, u, :], lhsT=kT[ks][hh * 64:(hh + 1) * 64, :], rhs=qTi[hh * 64:(hh + 1) * 64, :],
                        start=True, stop=True), [kT[ks], qTi], [ps_])
                S.op("act", lambda e, hh=hh, ps_=ps_: e.activation(out=P_[:, hh, :, :], in_=ps_[:], func=AF.Exp), [ps_], [P_])
            S.op("dve", lambda e: e.tensor_tensor(
                out=P_[:].rearrange("p h u a -> p h (u a)"), in0=P_[:].rearrange("p h u a -> p h (u a)"),
                in1=mab[:, :].unsqueeze(1).broadcast_to([128, 2, 256]), op=ALU.mult), [P_, mab], [P_])

        def back(k):
            grp, b = jobs[k]
            P_ = Pt[k % 2]
            po = pO[k % 2]
            for hh in range(2):
                for u in range(2):
                    vb = grp["vbuf"][b + u]
                    S.op("pe", lambda e, hh=hh, u=u, vb=vb: e.matmul(
                        po[:, hh, :], lhsT=P_[:, hh, u, :], rhs=vb[:, hh, :],
                        start=(u == 0), stop=(u == 1)), [P_, vb], [po])
            o = nz[k % 2]
            S.op("act", lambda e: e.activation(out=o[:], in_=po[:].rearrange("p h c -> p (h c)"), func=AF.Copy), [po], [o])
            S.dma("pool", grp["nzv"][grp["r"], b * 128:(b + 1) * 128, :], o[:], o, False)

        nj = len(jobs)
        load(0)
        if nj > 1:
            load(1)
        front(0)
        for k in range(nj):
            if k + 2 < nj:
                load(k + 2)
            if k + 1 < nj:
                front(k + 1)
            back(k)
        ph.close()

    def phase_O(self, l, xin):
        S, din = self.S, self.din
        ph = Phase(self, "O%d" % l)
        NB = 4
        wo = ph.sb([128, 6, D], BF16, "wo")
        stage = [ph.sb([128, D], F32, "stg") for _ in range(2)]
        Ga = ph.sb([128, 64], F32, "Ga")
        S.dma("sp", Ga[:], din["a_norm_g"][l:l + 1, :].broadcast_to([128, 64]), Ga, True)
        NSL = 3
        xt = [ph.sb([128, NB, D], F32, "xt") for _ in range(NSL)]
        self.load_weight_simple(wo, din["w_out"][l], None, [stage[0], stage[1], (xt[1], xt[1][:, 0, :]), (xt[2], xt[2][:, 0, :])], D)
        ofb = [ph.sb([128, NB, 2, 256], F32, "ofb") for _ in range(NSL)]
        gt = [ph.sb([128, NB, 256], BF16, "gt") for _ in range(NSL)]
        nz = [ph.sb([128, NB, 3, 130], F32, "nz") for _ in range(NSL)]
        yc = [ph.sb([128, NB, 768], BF16, "yc") for _ in range(NSL)]
        osum = ph.sb([128, NB, 256], F32, "osum"); osq = ph.sb([128, NB, 256], F32, "osq")
        ssq = ph.sb([128, NB * 4], F32); rs = ph.sb([128, NB * 4], F32)
        nsum = ph.sb([128, NB, 130], F32, "nsum"); rden = ph.sb([128, NB, 2], F32)
        ycT = [ph.sb([128, 6, 128], BF16, "ycT") for _ in range(2)]
        ht = [ph.sb([128, NB, D], F32, "ht") for _ in range(2)]
        pT = [ph.ps([128, 6, 128], BF16, "pT") for _ in range(2)]
        pW = [ph.ps([128, 512], F32, "pW") for _ in range(2)]
        idb = self.idb
        blocks = [(si, b) for si, T in enumerate(self.seqs) for b in range(T // (128 * NB))]

        def load(n):
            si, b = blocks[n]
            rows = slice(b * NB * 128, (b + 1) * NB * 128)
            k = n % NSL
            S.dma("sp", xt[k][:], xin[si][rows, :].rearrange("(t p) c -> p t c", p=128), xt[k], True)
            for d in range(2):
                S.dma("sp", ofb[k][:, :, d, :], self.OFB[si][d, rows, :].rearrange("(t p) c -> p t c", p=128), ofb[k], True)
            S.dma("sp", gt[k][:], self.PA[si][rows, O_GATE:O_GATE + 256].rearrange("(t p) c -> p t c", p=128), gt[k], True)
            for g in range(3):
                S.dma("sp", nz[k][:, :, g, :], self.NZ[si][g, rows, :].rearrange("(t p) c -> p t c", p=128), nz[k], True)
            S.dma("sp", yc[k][:, :, 256:640], self.YB[si][rows, :].rearrange("(t p) c -> p t c", p=128), yc[k], True)

        def prep_steps(n):
            k = n % NSL
            of, g_, nz_, y = ofb[k], gt[k], nz[k], yc[k]
            o3 = osum[:].rearrange("p t (h c) -> p (t h) c", h=4)
            n4 = nsum[:].rearrange("p t (h c) -> p t h c", h=2)
            return [
                lambda: S.op("dve", lambda e: e.tensor_tensor(out=osum[:], in0=of[:, :, 0, :], in1=of[:, :, 1, :], op=ALU.add), [of], [osum]),
                lambda: S.op("act", lambda e: e.activation(out=osq[:], in_=osum[:], func=AF.Square), [osum], [osq]),
                lambda: S.op("dve", lambda e: e.tensor_reduce(out=ssq[:], in_=osq[:].rearrange("p t (h c) -> p (t h) c", h=4), axis=AX.X, op=ALU.add), [osq], [ssq]),
                lambda: S.op("dve", lambda e: e.tensor_tensor(out=nsum[:], in0=nz_[:, :, 0, :], in1=nz_[:, :, 1, :], op=ALU.add), [nz_], [nsum]),
                lambda: self.rstd_from_ss(ssq, rs, 64),
                lambda: S.op("dve", lambda e: e.tensor_tensor(out=nsum[:], in0=nsum[:], in1=nz_[:, :, 2, :], op=ALU.add), [nz_, nsum], [nsum]),
                lambda: S.op("dve", lambda e: e.tensor_tensor(out=o3, in0=o3, in1=rs[:, :].unsqueeze(2).broadcast_to([128, NB * 4, 64]), op=ALU.mult), [osum, rs], [osum]),
                lambda: S.op("dve", lambda e: e.reciprocal(out=rden[:], in_=n4[:, :, :, 64]), [nsum], [rden]),
                lambda: S.op("dve", lambda e: e.tensor_tensor(out=o3, in0=o3, in1=Ga[:, :].unsqueeze(1).broadcast_to([128, NB * 4, 64]), op=ALU.mult), [osum, Ga], [osum]),
                lambda: S.op("dve", lambda e: e.tensor_tensor(out=y[:, :, 640:768].rearrange("p t (h c) -> p t h c", h=2), in0=n4[:, :, :, 0:64],
                                                              in1=rden[:, :, :].unsqueeze(3).broadcast_to([128, NB, 2, 64]), op=ALU.mult), [nsum, rden], [y]),
                lambda: S.op("dve", lambda e: e.tensor_tensor(out=y[:, :, 0:256], in0=osum[:], in1=g_[:], op=ALU.mult), [osum, g_], [y]),
            ]

        def prep(n):
            for st in prep_steps(n):
                st()

        def fin(n):
            si, b = blocks[n]
            rows = slice(b * NB * 128, (b + 1) * NB * 128)
            k = n % NSL
            x, y = xt[k], yc[k]
            h = ht[n % 2]
            nxt = prep_steps(n + 1) if n + 1 < nt else []
            per = -(-len(nxt) // NB)

            def tr(t):
                ycTn = ycT[t % 2]
                pt_ = pT[t % 2]
                for c in range(6):
                    S.op("pe", lambda e, c=c, t=t: e.transpose(out=pt_[:, c, :], in_=y[:, t, c * 128:(c + 1) * 128], identity=idb[:]), [y, idb], [pt_])
                S.op("act", lambda e: e.activation(out=ycTn[:], in_=pt_[:], func=AF.Copy), [pt_], [ycTn])

            tr(0)
            for t in range(NB):
                if t + 1 < NB:
                    tr(t + 1)
                ycTn = ycT[t % 2]
                for nc_ in range(2):
                    p = pW[nc_]
                    for kc in range(6):
                        S.op("pe", lambda e, p=p, kc=kc, nc_=nc_, ycTn=ycTn: e.matmul(p[:], lhsT=ycTn[:, kc, :], rhs=wo[:, kc, nc_ * 512:(nc_ + 1) * 512], start=(kc == 0), stop=(kc == 5)), [ycTn, wo], [p])
                    S.op("dve", lambda e, p=p, nc_=nc_, t=t: e.tensor_tensor(out=h[:, t, nc_ * 512:(nc_ + 1) * 512], in0=p[:], in1=x[:, t, nc_ * 512:(nc_ + 1) * 512], op=ALU.add), [p, x], [h])
                for st in nxt[t * per:(t + 1) * per]:
                    st()
            S.dma("pool", self.H[si][rows, :].rearrange("(t p) c -> p t c", p=128), h[:], h, False)

        nt = len(blocks)
        load(0)
        if nt > 1:
            load(1)
        prep(0)
        for n in range(nt):
            if n + 2 < nt:
                load(n + 2)
            fin(n)
        ph.close()

    def phase_F(self, l, xout):
        S, din = self.S, self.din
        ph = Phase(self, "F%d" % l)
        wg = ph.sb([128, 8, DFF], BF16, "wg")
        wu = ph.sb([128, 8, DFF], BF16, "wu")
        wd = ph.sb([128, NFF, D], BF16, "wd")
        stage = [ph.sb([128, 1024], F32, "stg") for _ in range(2)]
        g2 = ph.sb([128, 8], F32, "g2")
        S.dma("sp", g2[:], din["norm2_g"][l].rearrange("(c p) -> p c", p=128), g2, True, allow_slow_non_contiguous=True)
        NB = 2
        ht = [ph.sb([128, NB, D], F32, "ht") for _ in range(2)]
        stage4 = [stage[0], stage[1], (ht[0], ht[0][:, 0, :]), (ht[1], ht[1][:, 0, :])]
        self.load_weight_simple(wg, din["w_gate"][l], g2, stage4, DFF)
        self.load_weight_simple(wu, din["w_up"][l], g2, stage4, DFF)
        self.load_weight_simple(wd, din["w_down"][l], None, stage4, D)
        junk = ph.sb([128, D], BF16, "junk")
        ss = ph.sb([128, 1], F32); rstd = ph.sb([128, 1], F32)
        hb = ph.sb([128, D], BF16, "hb")
        hT = [ph.sb([128, 8, NB * 128], BF16, "hT") for _ in range(2)]
        sgt = [ph.sb([128, NB * 128], F32, "sgt") for _ in range(2)]
        aT = [ph.sb([128, NB * 128], BF16, "aT") for _ in range(2)]
        yo = [ph.sb([128, D], F32, "yo") for _ in range(2)]
        pT = ph.ps([128, 8, 128], BF16, "pT")
        pG = [ph.ps([128, 2, NB * 128], F32, "pG") for _ in range(2)]
        pD = [ph.ps([128, 2, 512], F32, "pD") for _ in range(NB)]
        idb = self.idb
        blocks = [(si, b) for si, T in enumerate(self.seqs) for b in range(T // (128 * NB))]

        def load(n):
            si, b = blocks[n]
            h = ht[n % 2]
            S.dma("sp", h[:], self.H[si][b * NB * 128:(b + 1) * NB * 128, :].rearrange("(t p) c -> p t c", p=128), h, True)

        def front(n):
            h = ht[n % 2]
            hTn = hT[n % 2]
            for t in range(NB):
                S.op("act", lambda e, t=t: e.activation(out=junk[:], in_=h[:, t, :], func=AF.Square, accum_out=ss[:]), [h], [junk, ss])
                self.rstd_from_ss(ss, rstd, D)
                S.op("dve", lambda e, t=t: e.tensor_scalar(out=hb[:], in0=h[:, t, :], scalar1=rstd[:], scalar2=None, op0=ALU.mult), [h, rstd], [hb])
                for c in range(8):
                    S.op("pe", lambda e, c=c: e.transpose(out=pT[:, c, :], in_=hb[:, c * 128:(c + 1) * 128], identity=idb[:]), [hb, idb], [pT])
                S.op("act", lambda e, t=t: e.activation(out=hTn[:, :, t * 128:(t + 1) * 128], in_=pT[:], func=AF.Copy), [pT], [hTn])

        def front_tile(n, t):
            h = ht[n % 2]
            hTn = hT[n % 2]
            S.op("act", lambda e, t=t: e.activation(out=junk[:], in_=h[:, t, :], func=AF.Square, accum_out=ss[:]), [h], [junk, ss])
            self.rstd_from_ss(ss, rstd, D)
            S.op("dve", lambda e, t=t: e.tensor_scalar(out=hb[:], in0=h[:, t, :], scalar1=rstd[:], scalar2=None, op0=ALU.mult), [h, rstd], [hb])
            for c in range(8):
                S.op("pe", lambda e, c=c: e.transpose(out=pT[:, c, :], in_=hb[:, c * 128:(c + 1) * 128], identity=idb[:]), [hb, idb], [pT])
            S.op("act", lambda e, t=t: e.activation(out=hTn[:, :, t * 128:(t + 1) * 128], in_=pT[:], func=AF.Copy), [pT], [hTn])

        def ffn(n):
            si, b = blocks[n]
            h = ht[n % 2]
            hTn = hT[n % 2]

            def gu(f):
                pg = pG[f % 2]
                for j, w in enumerate((wg, wu)):
                    for kc in range(8):
                        S.op("pe", lambda e, j=j, w=w, kc=kc, f=f, pg=pg: e.matmul(pg[:, j, :], lhsT=w[:, kc, f * 128:(f + 1) * 128], rhs=hTn[:, kc, :], start=(kc == 0), stop=(kc == 7)), [w, hTn], [pg])
                sg_ = sgt[f % 2]
                a_ = aT[f % 2]
                S.op("act", lambda e, pg=pg, sg_=sg_: e.activation(out=sg_[:], in_=pg[:, 0, :], func=AF.Silu), [pg], [sg_])
                S.op("dve", lambda e, pg=pg, sg_=sg_, a_=a_: e.tensor_tensor(out=a_[:], in0=pg[:, 1, :], in1=sg_[:], op=ALU.mult), [pg, sg_], [a_])

            def dn(f):
                a_ = aT[f % 2]
                for t in range(NB):
                    for nc_ in range(2):
                        S.op("pe", lambda e, t=t, nc_=nc_, f=f, a_=a_: e.matmul(pD[t][:, nc_, :], lhsT=a_[:, t * 128:(t + 1) * 128], rhs=wd[:, f, nc_ * 512:(nc_ + 1) * 512], start=(f == 0), stop=(f == NFF - 1)), [a_, wd], [pD[t]])

            gu(0)
            for f in range(1, NFF):
                gu(f)
                dn(f - 1)
                if n + 1 < nb:
                    if f == 8:
                        front_tile(n + 1, 0)
                    elif f == 15:
                        front_tile(n + 1, 1)
            dn(NFF - 1)
            for t in range(NB):
                y = yo[t % 2]
                S.op("dve", lambda e, t=t, y=y: e.tensor_tensor(out=y[:], in0=pD[t][:].rearrange("p a b -> p (a b)"), in1=h[:, t, :], op=ALU.add), [pD[t], h], [y])
                r0 = (b * NB + t) * 128
                S.dma("pool", xout[si][r0:r0 + 128, :], y[:], y, False)

        nb = len(blocks)
        load(0)
        front(0)
        for n in range(nb):
            if n + 1 < nb:
                load(n + 1)
            ffn(n)
        ph.close()


def make_consts(Tmax):
    c = {}
    c["c_ident"] = np.eye(128, dtype=np.float32)
    pos = np.arange(Tmax, dtype=np.float32)
    inv = (10000.0 ** (-np.arange(32, dtype=np.float32) / 32)).astype(np.float32)
    ang = pos[:, None] * inv[None, :]
    cos = np.cos(ang).astype(np.float32); sin = np.sin(ang).astype(np.float32)
    c["c_rope"] = np.concatenate([cos, cos, -sin, sin], axis=1).astype(np.float32)
    ci = np.arange(128)[:, None]; ai = np.arange(128)[None, :]
    c["c_mab"] = np.concatenate([(ci >= ai), (ci <= ai)], axis=1).astype(np.float32)
    s = np.arange(128)[:, None]; t = np.arange(128)[None, :]
    same = (s // 64) == (t // 64)
    sL = s % 64; tL = t % 64
    c["c_mh"] = np.concatenate([same & (sL <= tL), same & (sL >= tL)], axis=1).astype(np.float32)
    Ls = [
        same * ((sL <= tL).astype(np.float32) - (sL <= 31)),
        same * (sL <= tL),
        same * (sL > tL),
        same * ((sL >= tL).astype(np.float32) - (sL >= 32)),
        same * (sL >= tL),
        same * (sL < tL),
    ]
    c["c_L"] = np.concatenate([np.asarray(m, dtype=np.float32) for m in Ls], axis=1)
    c["c_sel"] = np.stack([(np.arange(128) // 64 == 0), (np.arange(128) // 64 == 1)], axis=1).astype(np.float32)
    return c


W_NAMES = ("norm1_g", "w_in", "lb_raw", "a_norm_g", "b_qn_g", "b_kn_g", "b_sink", "c_qn_g", "c_kn_g",
           "w_out", "norm2_g", "w_gate", "w_up", "w_down")

_CACHE = {}


def kernel(**inputs):
    xp = np.ascontiguousarray(np.asarray(inputs["x_prompt"], dtype=np.float32))
    xs = np.ascontiguousarray(np.asarray(inputs["x_sample"], dtype=np.float32))
    n = 8
    TS, TP = xs.shape[1], xp.shape[1]
    if "prog" not in _CACHE:
        P = Prog([TS, TP], 4)
        P.build()
        _CACHE["prog"] = P
    P = _CACHE["prog"]
    consts = make_consts(max(TS, TP))
    wts = {k: np.ascontiguousarray(np.asarray(inputs[k], dtype=np.float32)) for k in W_NAMES}
    in_maps = []
    for c in range(n):
        m = {"x0": xs[c], "x1": xp[c % xp.shape[0]]}
        m.update(wts)
        m.update(consts)
        in_maps.append(m)
    res = run_bass_kernel_spmd(P.nc, in_maps, core_ids=list(range(n)))
    y_s = np.stack([res.results[c]["y0"] for c in range(n)], axis=0)
    y_p = np.stack([res.results[c]["y1"] for c in range(xp.shape[0])], axis=0)
    return (y_p.astype(np.float32), y_s.astype(np.float32))
```
